# Optimizing a Trainium2 kernel written in Bass

```python
import jax
import jax.numpy as jnp
from jax import lax
import numpy as np


D_MODEL = 1024
BATCH = 8
SEQ = 2048
DEPTH = 4

HEAD_DIM = 64
FOX_HEADS = 8
SB_HEADS = 8
FOX_WIDTH = FOX_HEADS * HEAD_DIM
SB_WIDTH = SB_HEADS * HEAD_DIM
CONV_WIDTH = 512
CONV_K = 3
N_BRANCH = 3
D_FF = 2816
Q_BLOCK = 128
NORM_EPS = 1e-6
NEG_INF = -1e30

SPLIT_POINTS = [
    3 * CONV_WIDTH,
    3 * CONV_WIDTH + 3 * FOX_WIDTH,
    3 * CONV_WIDTH + 3 * FOX_WIDTH + FOX_HEADS,
    3 * CONV_WIDTH + 3 * FOX_WIDTH + FOX_HEADS + 3 * SB_WIDTH,
]
D_IN = 3 * CONV_WIDTH + 3 * FOX_WIDTH + FOX_HEADS + 3 * SB_WIDTH + N_BRANCH * D_MODEL

kernel_name = 'hybrid_gatedconv_fox_stickbreak_convglu'


def rmsnorm(x, g):
    xf = x.astype(jnp.float32)
    y = xf * lax.rsqrt(jnp.mean(xf * xf, axis=-1, keepdims=True) + NORM_EPS)
    return (y * g.astype(jnp.float32)).astype(x.dtype)


def causal_dwconv(x, w):
    k = w.shape[0]
    return lax.conv_general_dilated(
        x, w[:, None, :].astype(x.dtype), window_strides=(1,), padding=[(k - 1, 0)],
        dimension_numbers=('NWC', 'WIO', 'NWC'), feature_group_count=x.shape[-1])


def _heads(t, n):
    b, s, _ = t.shape
    return t.reshape(b, s, n, HEAD_DIM).transpose(0, 2, 1, 3)


def _to_blocks(t):
    b, h, s = t.shape[:3]
    t = t.reshape(b, h, s // Q_BLOCK, Q_BLOCK, *t.shape[3:])
    return jnp.moveaxis(t, 2, 0)


def _from_blocks(o):
    nb, b, h, q, d = o.shape
    return o.transpose(1, 0, 3, 2, 4).reshape(b, nb * q, h * d)


def short_conv_mixer(b_gate, c_gate, h, conv_w):
    return b_gate * causal_dwconv(c_gate * h, conv_w)


def forgetting_attention(q, k, v, log_f, qn_g, kn_g):
    q = rmsnorm(_heads(q, FOX_HEADS), qn_g)
    k = rmsnorm(_heads(k, FOX_HEADS), kn_g)
    v = _heads(v, FOX_HEADS)
    c = lax.cumsum(log_f, axis=1).transpose(0, 2, 1)
    pos = jnp.arange(q.shape[2])
    scale = HEAD_DIM ** -0.5

    def block(args):
        qb, cqb, qpos = args
        logits = (jnp.einsum('bhqd,bhkd->bhqk', qb, k).astype(jnp.float32) * scale
                  + cqb[..., None] - c[:, :, None, :])
        logits = jnp.where(pos[None, :] <= qpos[:, None], logits, NEG_INF)
        p = jax.nn.softmax(logits, axis=-1)
        return jnp.einsum('bhqk,bhkd->bhqd', p.astype(v.dtype), v)

    out = lax.map(block, (_to_blocks(q), _to_blocks(c), pos.reshape(-1, Q_BLOCK)))
    return _from_blocks(out)


def stick_breaking_attention(q, k, v):
    q = _heads(q, SB_HEADS)
    k = _heads(k, SB_HEADS)
    v = _heads(v, SB_HEADS)
    pos = jnp.arange(q.shape[2])
    scale = HEAD_DIM ** -0.5

    def block(args):
        qb, qpos = args
        z = jnp.einsum('bhqd,bhkd->bhqk', qb, k).astype(jnp.float32) * scale
        strict = pos[None, :] < qpos[:, None]
        log_1m = jnp.where(strict, jax.nn.log_sigmoid(-z), 0.0)
        suffix = lax.cumsum(log_1m, axis=log_1m.ndim - 1, reverse=True) - log_1m
        a = jnp.where(strict, jnp.exp(jax.nn.log_sigmoid(z) + suffix), 0.0)
        return jnp.einsum('bhqk,bhkd->bhqd', a.astype(v.dtype), v)

    out = lax.map(block, (_to_blocks(q), pos.reshape(-1, Q_BLOCK)))
    return _from_blocks(out)


def setup_inputs(seed: int = 0) -> dict:
    key = jax.random.key(seed)
    ks = jax.random.split(key, 17)
    f32 = jnp.float32
    nrm = lambda k, shape, s: jax.random.normal(k, shape, f32) * s
    return {
        'x': nrm(ks[0], (BATCH, SEQ, D_MODEL), 1.0),
        'norm1_g': 1.0 + nrm(ks[1], (DEPTH, D_MODEL), 0.05),
        'w_in': nrm(ks[2], (DEPTH, D_MODEL, D_IN), D_MODEL ** -0.5),
        'fox_f_bias': 2.0 + nrm(ks[3], (DEPTH, FOX_HEADS), 0.5),
        'gate_bias': nrm(ks[4], (DEPTH, N_BRANCH * D_MODEL), 0.1),
        'conv_w': nrm(ks[5], (DEPTH, CONV_K, CONV_WIDTH), CONV_K ** -0.5),
        'fox_q_norm_g': 1.0 + nrm(ks[6], (DEPTH, HEAD_DIM), 0.05),
        'fox_k_norm_g': 1.0 + nrm(ks[7], (DEPTH, HEAD_DIM), 0.05),
        'w_proj_conv': nrm(ks[8], (DEPTH, CONV_WIDTH, D_MODEL), CONV_WIDTH ** -0.5),
        'w_proj_fox': nrm(ks[9], (DEPTH, FOX_WIDTH, D_MODEL), FOX_WIDTH ** -0.5),
        'w_proj_sb': nrm(ks[10], (DEPTH, SB_WIDTH, D_MODEL), SB_WIDTH ** -0.5),
        'w_out': nrm(ks[11], (DEPTH, D_MODEL, D_MODEL), D_MODEL ** -0.5),
        'norm2_g': 1.0 + nrm(ks[12], (DEPTH, D_MODEL), 0.05),
        'w_up': nrm(ks[13], (DEPTH, D_MODEL, 2 * D_FF), D_MODEL ** -0.5),
        'ffn_conv_w': nrm(ks[14], (DEPTH, CONV_K, D_FF), CONV_K ** -0.5),
        'ffn_conv_b': nrm(ks[15], (DEPTH, D_FF), 0.02),
        'w_down': nrm(ks[16], (DEPTH, D_FF, D_MODEL), D_FF ** -0.5),
    }


def reference(x, norm1_g, w_in, fox_f_bias, gate_bias, conv_w, fox_q_norm_g, fox_k_norm_g,
              w_proj_conv, w_proj_fox, w_proj_sb, w_out, norm2_g, w_up, ffn_conv_w,
              ffn_conv_b, w_down):
    for l in range(DEPTH):
        hn = rmsnorm(x, norm1_g[l])
        proj = hn @ w_in[l]
        conv_bch, fox_qkv, fox_f, sb_qkv, gate_logits = jnp.split(proj, SPLIT_POINTS, axis=-1)

        cb, cc, ch = jnp.split(conv_bch, 3, axis=-1)
        y_conv = short_conv_mixer(cb, cc, ch, conv_w[l]) @ w_proj_conv[l]

        fq, fk, fv = jnp.split(fox_qkv, 3, axis=-1)
        log_f = jax.nn.log_sigmoid(fox_f.astype(jnp.float32) + fox_f_bias[l].astype(jnp.float32))
        y_fox = forgetting_attention(fq, fk, fv, log_f, fox_q_norm_g[l], fox_k_norm_g[l]) @ w_proj_fox[l]

        sq, sk, sv = jnp.split(sb_qkv, 3, axis=-1)
        y_sb = stick_breaking_attention(sq, sk, sv) @ w_proj_sb[l]

        g_conv, g_fox, g_sb = jnp.split(jax.nn.sigmoid(gate_logits + gate_bias[l]), 3, axis=-1)
        x = x + (g_conv * y_conv + g_fox * y_fox + g_sb * y_sb) @ w_out[l]

        hn = rmsnorm(x, norm2_g[l])
        u_gate, u_val = jnp.split(hn @ w_up[l], 2, axis=-1)
        act = jax.nn.silu(causal_dwconv(u_gate, ffn_conv_w[l]) + ffn_conv_b[l])
        x = x + (act * u_val) @ w_down[l]
    return x
```

```python
import numpy as np
from contextlib import ExitStack
import concourse.bass as bass
import concourse.mybir as mybir
from concourse.bass_utils import run_bass_kernel_spmd

F32 = mybir.dt.float32
BF16 = mybir.dt.bfloat16
AF = mybir.ActivationFunctionType
ALU = mybir.AluOpType
EPS = 1e-6


class Cfg:
    def __init__(self, S=2048, D=1024, DEPTH=4, NP=4, NCC=4, NFF=22, TG=1024):
        self.S, self.D, self.DEPTH, self.NP, self.NCC, self.NFF = S, D, DEPTH, NP, NCC, NFF
        self.KC = D // 128
        self.NT = S // 128
        self.NTB = S // 512
        self.NH = 2 * NP
        self.CW = NCC * 128
        self.FW = NP * 128
        self.DFF = NFF * 128
        self.DIN = 3 * self.CW + 3 * self.FW + self.NH + 3 * self.FW + 3 * D
        self.TG = min(S, TG)
        self.NG = S // self.TG
        self.HK = (NFF + 1) // 2
        self.WSZ = max(self.KC, self.HK, NCC, NP) * 128
        self.OB, self.OC, self.OH = 0, NCC, 2 * NCC
        self.OFQ = 3 * NCC
        self.OFK = self.OFQ + NP
        self.OFV = self.OFK + NP
        self.OSQ = self.OFV + NP
        self.OSK = self.OSQ + NP
        self.OSV = self.OSK + NP
        self.OG = self.OSV + NP
        self.NBLK = self.OG + 3 * self.KC
        o = 0
        self.P_G1 = o; o += DEPTH * self.KC
        self.P_G2 = o; o += DEPTH * self.KC
        self.P_GB = o; o += DEPTH * 3 * self.KC
        self.P_CW = o; o += DEPTH * NCC * 3
        self.P_FB = o; o += DEPTH * self.NT * self.NH
        self.P_GQ = o; o += DEPTH
        self.P_GK = o; o += DEPTH
        self.P_FW = o; o += DEPTH * NFF * 3
        self.P_FC = o; o += DEPTH * NFF
        self.NPRM = o


def _blocks(W, col_starts):
    K = W.shape[0]
    kc = K // 128
    out = np.empty((len(col_starts), 128, kc * 128), np.float32)
    Wr = W.reshape(kc, 128, W.shape[1])
    for i, c0 in enumerate(col_starts):
        out[i] = Wr[:, :, c0:c0 + 128].transpose(1, 0, 2).reshape(128, kc * 128)
    return out


def host_layout(cfg, inp):
    c = cfg
    L = c.DEPTH
    d = {}
    w_in = np.asarray(inp["w_in"], np.float32)
    fcol = 3 * c.CW + 3 * c.FW
    sb0 = fcol + c.NH
    g0 = sb0 + 3 * c.FW
    starts = []
    for grp in range(3):
        starts += [grp * c.CW + i * 128 for i in range(c.NCC)]
    for grp in range(3):
        starts += [3 * c.CW + grp * c.FW + i * 128 for i in range(c.NP)]
    for grp in range(3):
        starts += [sb0 + grp * c.FW + i * 128 for i in range(c.NP)]
    for br in range(3):
        starts += [g0 + br * c.D + i * 128 for i in range(c.KC)]
    assert len(starts) == c.NBLK
    d["win"] = np.stack([_blocks(w_in[l], starts) for l in range(L)])
    wf = w_in[:, :, fcol:fcol + c.NH].reshape(L, c.KC, 128, c.NH).transpose(0, 2, 1, 3)
    d["wf"] = np.ascontiguousarray(wf).reshape(L, 128, c.KC * c.NH)
    oc_starts = [i * 128 for i in range(c.KC)]
    d["wpc"] = np.stack([_blocks(np.asarray(inp["w_proj_conv"][l], np.float32), oc_starts) for l in range(L)])
    d["wpf"] = np.stack([_blocks(np.asarray(inp["w_proj_fox"][l], np.float32), oc_starts) for l in range(L)])
    d["wps"] = np.stack([_blocks(np.asarray(inp["w_proj_sb"][l], np.float32), oc_starts) for l in range(L)])
    d["wout"] = np.stack([_blocks(np.asarray(inp["w_out"][l], np.float32), oc_starts) for l in range(L)])
    d["wup"] = np.stack([_blocks(np.asarray(inp["w_up"][l], np.float32), [i * 128 for i in range(2 * c.NFF)]) for l in range(L)])
    d["wdn"] = np.stack([_blocks(np.asarray(inp["w_down"][l], np.float32), oc_starts) for l in range(L)])
    prm = np.zeros((128, c.NPRM), np.float32)

    def pm(v, n):
        return np.asarray(v, np.float32).reshape(L, n, 128).transpose(2, 0, 1).reshape(128, L * n)
    prm[:, c.P_G1:c.P_G1 + L * c.KC] = pm(inp["norm1_g"], c.KC)
    prm[:, c.P_G2:c.P_G2 + L * c.KC] = pm(inp["norm2_g"], c.KC)
    prm[:, c.P_GB:c.P_GB + L * 3 * c.KC] = pm(inp["gate_bias"], 3 * c.KC)
    cw = np.asarray(inp["conv_w"], np.float32).reshape(L, 3, c.NCC, 128).transpose(3, 0, 2, 1)
    prm[:, c.P_CW:c.P_CW + L * c.NCC * 3] = cw.reshape(128, -1)
    fb = np.asarray(inp["fox_f_bias"], np.float32)
    prm[:, c.P_FB:c.P_FB + L * c.NT * c.NH] = np.broadcast_to(fb[None, :, None, :], (128, L, c.NT, c.NH)).reshape(128, -1)
    gq = np.asarray(inp["fox_q_norm_g"], np.float32)
    gk = np.asarray(inp["fox_k_norm_g"], np.float32)
    prm[:, c.P_GQ:c.P_GQ + L] = np.concatenate([gq, gq], axis=1).T
    prm[:, c.P_GK:c.P_GK + L] = np.concatenate([gk, gk], axis=1).T
    fw = np.asarray(inp["ffn_conv_w"], np.float32).reshape(L, 3, c.NFF, 128).transpose(3, 0, 2, 1)
    prm[:, c.P_FW:c.P_FW + L * c.NFF * 3] = fw.reshape(128, -1)
    prm[:, c.P_FC:c.P_FC + L * c.NFF] = pm(inp["ffn_conv_b"], c.NFF)
    d["prm"] = prm
    i = np.arange(128)
    le = (i[:, None] <= i[None, :]).astype(np.float32)
    lt = (i[:, None] < i[None, :]).astype(np.float32)
    cst = np.zeros((128, 12 * 128), np.float32)
    cst[:, 0:128] = np.eye(128)
    cst[:, 128:256] = le
    cst[:, 256:384] = 1.0
    cst[:, 384:512] = np.where(le > 0, 1e30, -1e4)
    cst[:, 512:640] = lt
    cst[:, 640:768] = 1.0
    bd = np.zeros((128, 128), np.float32); bd[:64, :64] = 1; bd[64:, 64:] = 1
    cst[:, 768:896] = bd
    cst[:, 896:1024] = -(i[:, None] >= i[None, :]).astype(np.float32)
    cst[:, 1024:1152] = -(i[:, None] < i[None, :]).astype(np.float32)
    cst[:, 1152:1280] = 0.0
    cst[:, 1280:1408] = np.eye(128)
    cst[:, 1408:1536] = np.where(le > 0, 0.0, -1e4)
    d["cst"] = cst
    return d


class Buf:
    __slots__ = ("w", "r", "name")

    def __init__(self, name=""):
        self.w = []
        self.r = []
        self.name = name


class Tracker:
    ROT = 2000

    def __init__(self, nc, es):
        self.nc, self.es = nc, es
        self.E = {"pe": nc.tensor, "act": nc.scalar, "dve": nc.vector, "pool": nc.gpsimd, "sp": nc.sync}
        self.sem, self.cnt, self.gen, self.key = {}, {}, {}, {}
        self.seen = {k: {} for k in self.E}
        self.nsem = 0
        self.ninst = 0
        self.dsems = []
        self.prev = {}
        for k in self.E:
            self._newsem(k)

    def _newsem(self, k):
        g = self.gen.get(k, -1) + 1
        if g > 0:
            self.prev[k] = (self.key[k], self.sem[k], self.cnt[k])
        self.gen[k] = g
        self.sem[k] = self.es.enter_context(self.nc.semaphore(f"s_{k}_{g}"))
        self.cnt[k] = 0
        self.key[k] = f"{k}:{g}"
        self.nsem += 1

    def _waits(self, e, reads, writes):
        need = {}

        def add(tok, raw):
            key, sem, val = tok
            own = key.split(":")[0] == e
            if own and e == "pe":
                return
            if need.get(key, (None, 0))[1] < val:
                need[key] = (sem, val)
        for b in reads:
            for t in b.w:
                add(t, True)
        for b in writes:
            for t in b.w:
                add(t, True)
            for t in b.r:
                add(t, False)
        eng = self.E[e]
        for key, (sem, val) in need.items():
            if self.seen[e].get(key, 0) >= val:
                continue
            eng.wait_ge(sem, val)
            self.seen[e][key] = val

    @staticmethod
    def _addr(b, tok):
        for i, t in enumerate(b.r):
            if t[0] == tok[0]:
                if t[2] < tok[2]:
                    b.r[i] = tok
                return
        b.r.append(tok)

    def op(self, e, fn, reads=(), writes=(), signal=True):
        self._waits(e, reads, writes)
        ins = fn(self.E[e])
        self.ninst += 1
        if signal:
            self.cnt[e] += 1
            ins.then_inc(self.sem[e], 1)
            tok = (self.key[e], self.sem[e], self.cnt[e])
        else:
            tok = (self.key[e], self.sem[e], self.cnt[e] + 1)
        for b in reads:
            self._addr(b, tok)
        for b in writes:
            b.w = [tok]
            b.r = []
        if signal and self.cnt[e] >= self.ROT:
            self._newsem(e)
        return tok

    def fence(self):
        for e, eng in self.E.items():
            toks = []
            for f in self.E:
                if f == e:
                    continue
                if self.cnt[f] > 0:
                    toks.append((self.key[f], self.sem[f], self.cnt[f]))
                elif f in self.prev:
                    toks.append(self.prev[f])
            for d in self.dsems:
                if d.cnt > 0:
                    toks.append((d.key, d.sem, d.cnt))
                elif d.prev is not None:
                    toks.append(d.prev)
            for key, sem, val in toks:
                if self.seen[e].get(key, 0) < val:
                    eng.wait_ge(sem, val)
                    self.seen[e][key] = val

    def dma(self, e, out, in_, dsem, reads=(), writes=()):
        self._waits(e, reads, writes)
        if dsem.cnt >= 1600:
            dsem.rotate()
        ins = self.E[e].dma_start(out=out, in_=in_)
        ins.then_inc(dsem.sem, 16)
        dsem.cnt += 16
        self.ninst += 1
        tok = (dsem.key, dsem.sem, dsem.cnt)
        for b in reads:
            self._addr(b, tok)
        for b in writes:
            b.w = [tok]
            b.r = []
        return tok


class DmaSem:
    _n = 0

    def __init__(self, tr, name):
        self.tr, self.name, self.prev = tr, name, None
        self._new()
        tr.dsems.append(self)

    def _new(self):
        DmaSem._n += 1
        self.sem = self.tr.es.enter_context(self.tr.nc.semaphore(f"d_{self.name}_{DmaSem._n}"))
        self.cnt = 0
        self.key = f"dma{DmaSem._n}:{self.name}"

    def rotate(self):
        self.prev = (self.key, self.sem, self.cnt)
        self._new()


class Ring:
    def __init__(self, nc, es, name, shape, dtype, n):
        self.t = [es.enter_context(nc.sbuf_tensor(f"{name}{i}", shape, dtype)) for i in range(n)]
        self.b = [Buf(f"{name}{i}") for i in range(n)]
        self.i = 0

    def get(self):
        k = self.i % len(self.t)
        self.i += 1
        return self.t[k], self.b[k]


def build_nc(cfg):
    c = cfg
    S, D, L, KC, NT, NTB, NH, NP, NCC, NFF = c.S, c.D, c.DEPTH, c.KC, c.NT, c.NTB, c.NH, c.NP, c.NCC, c.NFF
    nc = bass.Bass("TRN2", target_bir_lowering=False)
    dr = lambda n, sh, kind="ExternalInput": nc.dram_tensor(n, sh, F32, kind=kind).ap()
    x_d = dr("x", [S, D])
    win_d = dr("win", [L, c.NBLK, 128, KC * 128])
    wf_d = dr("wf", [L, 128, KC * NH])
    wpc_d = dr("wpc", [L, KC, 128, NCC * 128])
    wpf_d = dr("wpf", [L, KC, 128, NP * 128])
    wps_d = dr("wps", [L, KC, 128, NP * 128])
    wout_d = dr("wout", [L, KC, 128, KC * 128])
    wup_d = dr("wup", [L, 2 * NFF, 128, KC * 128])
    wdn_d = dr("wdn", [L, KC, 128, NFF * 128])
    prm_d = dr("prm", [128, c.NPRM])
    cst_d = dr("cst", [128, 1536])
    y_d = dr("y", [S, D], "ExternalOutput")

    with ExitStack() as es:
        tr = Tracker(nc, es)
        sb = lambda n, sh, dt: es.enter_context(nc.sbuf_tensor("sb_" + n, sh, dt))
        xT = sb("xT", [128, KC, S], F32)
        hnT = sb("hnT", [128, KC, S], BF16)
        xTb = [[Buf(f"xT{k}_{t}") for t in range(NTB)] for k in range(KC)]
        hnb = [Buf(f"hn{t}") for t in range(NTB)]
        prm = sb("prm", [128, c.NPRM], F32)
        cstf = sb("cstf", [128, 640], F32)
        cstb = sb("cstb", [128, 896], BF16)
        cst_b = Buf("cst")
        IDENT, TRIU, ONESF, CAP, MLT = (cstf[:, i * 128:(i + 1) * 128] for i in range(5))
        ONES, BD, NTI, NTS, ZEROS, IDENTB, NEGM = (cstb[:, i * 128:(i + 1) * 128] for i in range(7))
        ps = [es.enter_context(nc.psum_tensor(f"ps{i}", [128, 512], F32)) for i in range(8)]
        psb = [Buf(f"ps{i}") for i in range(8)]
        psi = [0]

        def bank(allowed=range(8)):
            allowed = list(allowed)
            k = allowed[psi[0] % len(allowed)]
            psi[0] += 1
            return ps[k], psb[k]
        NW = 4
        wsl = [sb(f"wsl{i}", [128, c.WSZ], BF16) for i in range(NW)]
        wslb = [Buf(f"wsl{i}") for i in range(NW)]
        wsem = [DmaSem(tr, f"w{i}") for i in range(NW)]
        wi = [0]

        wring = [list(zip(wsl, wslb, wsem))]

        def wload(src):
            ring = wring[0]
            k = wi[0] % len(ring)
            wi[0] += 1
            n = src.shape[1]
            t_, b_, s_ = ring[k]
            tr.dma("pool", t_[:, 0:n], src, s_, writes=[b_])
            return t_, b_

        def extra_slots(stack, tag, keep=1536):
            n_ = max(0, min(12, (nc.sbuf_bytes_remaining - keep) // (c.WSZ * 2)))
            ex = []
            for i in range(n_):
                t_ = stack.enter_context(nc.sbuf_tensor(f"wx_{tag}_{i}", [128, c.WSZ], BF16))
                ex.append((t_, Buf(), DmaSem(tr, f"wx{i}")))
            wring[0] = list(zip(wsl, wslb, wsem)) + ex
        f32r = Ring(nc, es, "f32r", [128, 512], F32, 6)
        bfr = Ring(nc, es, "bfr", [128, 512], BF16, 6)

        def mm(out, lhsT, rhs, start, stop, reads, writes, signal=True, skip=False):
            if skip:
                return tr.op("pe", lambda e: e.matmul(out, lhsT=lhsT, rhs=rhs, start=start, stop=stop, skip_group_check=True), reads, writes, signal)
            return tr.op("pe", lambda e: e.matmul(out, lhsT=lhsT, rhs=rhs, start=start, stop=stop), reads, writes, signal)

        def proj(w, wb, srcT, srcb, tcols, nk, out=None, outb=None):
            if out is None:
                pt, pb = bank()
                out = pt[:, 0:tcols.stop - tcols.start]
                outb = pb
            for k in range(nk):
                mm(out, w[:, k * 128:(k + 1) * 128], srcT[:, k, tcols], k == 0, k == nk - 1, [wb] + list(srcb), [outb], signal=(k == nk - 1))
            return out, outb

        csem = DmaSem(tr, "cst")
        tr.dma("sp", prm[:], prm_d, csem, writes=[cst_b])
        tr.dma("sp", cstf[:], cst_d[:, 0:640], csem, writes=[cst_b])
        with nc.sbuf_tensor("cb16tmp", [128, 896], F32) as cb16:
            tr.dma("sp", cb16[:], cst_d[:, 640:1536], csem, writes=[cst_b])
            tr.op("dve", lambda e: e.tensor_copy(cstb[:], cb16[:]), reads=[cst_b], writes=[cst_b])
        tr.fence()
        gqs = sb("gqs", [128, L], F32)
        tr.op("dve", lambda e: e.tensor_scalar_mul(gqs[:], prm[:, c.P_GQ:c.P_GQ + L], 0.125), reads=[cst_b], writes=[cst_b])
        epsc = sb("epsc", [128, 2], F32)
        tr.op("dve", lambda e: e.memset(epsc[:, 0:1], EPS), writes=[cst_b])
        tr.op("dve", lambda e: e.memset(epsc[:, 1:2], 1.0), writes=[cst_b])
        CB = [cst_b]

        with ExitStack() as es2:
            xin = [es2.enter_context(nc.sbuf_tensor(f"xin{i}", [128, D], F32)) for i in range(2)]
            xinb = [Buf(), Buf()]
            xsem = [DmaSem(tr, "xin0"), DmaSem(tr, "xin1")]
            for n in range(NT):
                k = n % 2
                tr.dma("sp", xin[k][:], x_d[n * 128:(n + 1) * 128, :], xsem[k], writes=[xinb[k]])
                for k0 in range(0, KC, 4):
                    nk = min(4, KC - k0)
                    pt, pb = bank()
                    for j in range(nk):
                        tr.op("pe", lambda e, j=j: e.transpose(pt[:, j * 128:(j + 1) * 128], xin[k][:, (k0 + j) * 128:(k0 + j + 1) * 128], IDENT),
                              reads=[xinb[k]] + CB, writes=[pb], signal=(j == nk - 1))
                    dst = xT[:, k0:k0 + nk, n * 128:(n + 1) * 128]
                    src = pt[:, 0:nk * 128].rearrange("p (j t) -> p j t", t=128)
                    wb_ = [xTb[k0 + j][n // 4] for j in range(nk)]
                    eng = "act" if (n + k0 // 4) % 2 == 0 else "dve"
                    if eng == "act":
                        tr.op("act", lambda e: e.copy(dst, src), reads=[pb], writes=wb_)
                    else:
                        tr.op("dve", lambda e: e.tensor_copy(dst, src), reads=[pb], writes=wb_)

        tr.fence()

        def norm(gcol0):
            for tb in range(NTB):
                tc_ = slice(tb * 512, (tb + 1) * 512)
                pt, pb = bank()
                for k in range(KC):
                    sq, sqb = bfr.get()
                    tr.op("act", lambda e: e.activation(sq[:], xT[:, k, tc_], AF.Square), reads=[xTb[k][tb]], writes=[sqb])
                    mm(pt[:], ONES, sq[:], k == 0, k == KC - 1, [sqb] + CB, [pb], signal=True)
                rs, rsb = f32r.get()
                tr.op("act", lambda e: e.activation(rs[:], pt[:], AF.Sqrt, bias=epsc[:, 0:1], scale=1.0 / D), reads=[pb] + CB, writes=[rsb])
                tr.op("dve", lambda e: e.reciprocal(rs[:], rs[:]), reads=[rsb], writes=[rsb])
                for k in range(KC):
                    tr.op("dve", lambda e: e.scalar_tensor_tensor(hnT[:, k, tc_], xT[:, k, tc_], prm[:, gcol0 + k:gcol0 + k + 1], rs[:], ALU.mult, ALU.mult),
                          reads=[xTb[k][tb], rsb] + CB, writes=[hnb[tb]])

        HN = hnb

        for l in range(L):
            norm(c.P_G1 + l * KC)
            with ExitStack() as esm:
                sbm = lambda n, sh, dt: esm.enter_context(nc.sbuf_tensor(f"{n}_{l}", sh, dt))
                convT = sbm("convT", [128, NCC, S], BF16)
                foxT = sbm("foxT", [128, NP, S], BF16)
                sbT = sbm("sbT", [128, NP, S], BF16)
                convb = [[Buf() for _ in range(NTB)] for _ in range(NCC)]
                foxb = [[Buf() for _ in range(NTB)] for _ in range(NP)]
                sbb = [[Buf() for _ in range(NTB)] for _ in range(NP)]
                with ExitStack() as esc:
                    u = esc.enter_context(nc.sbuf_tensor(f"u_{l}", [128, 2 + S], F32))
                    ub = [Buf() for _ in range(NTB)]
                    u0b = Buf()
                    tr.op("dve", lambda e: e.memset(u[:, 0:2], 0.0), writes=[u0b])
                    for cc in range(NCC):
                        wB, wBb = wload(win_d[l, c.OB + cc])
                        wC, wCb = wload(win_d[l, c.OC + cc])
                        wH, wHb = wload(win_d[l, c.OH + cc])
                        cwc = c.P_CW + (l * NCC + cc) * 3
                        for tb in range(NTB):
                            tc_ = slice(tb * 512, (tb + 1) * 512)
                            pC, pCb = proj(wC, wCb, hnT, [hnb[tb]], tc_, KC)
                            pH, pHb = proj(wH, wHb, hnT, [hnb[tb]], tc_, KC)
                            pB, pBb = proj(wB, wBb, hnT, [hnb[tb]], tc_, KC)
                            cs, csb = f32r.get()
                            tr.op("act", lambda e: e.copy(cs[:], pC), reads=[pCb], writes=[csb])
                            o = tb * 512
                            tr.op("dve", lambda e: e.tensor_tensor(u[:, 2 + o:2 + o + 512], cs[:], pH, ALU.mult), reads=[csb, pHb], writes=[ub[tb]])
                            t1, t1b = f32r.get()
                            prev = [ub[tb - 1]] if tb > 0 else [u0b]
                            tr.op("dve", lambda e: e.tensor_scalar_mul(t1[:], u[:, 2 + o:2 + o + 512], prm[:, cwc + 2:cwc + 3]), reads=[ub[tb]] + CB, writes=[t1b])
                            tr.op("dve", lambda e: e.scalar_tensor_tensor(t1[:], u[:, 1 + o:1 + o + 512], prm[:, cwc + 1:cwc + 2], t1[:], ALU.mult, ALU.add),
                                  reads=[ub[tb], t1b] + prev + CB, writes=[t1b])
                            tr.op("dve", lambda e: e.scalar_tensor_tensor(t1[:], u[:, o:o + 512], prm[:, cwc:cwc + 1], t1[:], ALU.mult, ALU.add),
                                  reads=[ub[tb], t1b] + prev + CB, writes=[t1b])
                            tr.op("dve", lambda e: e.tensor_tensor(convT[:, cc, tc_], t1[:], pB, ALU.mult), reads=[t1b, pBb], writes=[convb[cc][tb]])

                tr.fence()
                with ExitStack() as esa:
                    sba = lambda n, sh, dt: esa.enter_context(nc.sbuf_tensor(f"{n}_{l}", sh, dt))
                    qT = sba("qT", [128, S], BF16)
                    kT = sba("kT", [128, S], BF16)
                    Vp = sba("Vp", [128, NT, 128], BF16)
                    qb_ = [Buf() for _ in range(NTB)]
                    kb_ = [Buf() for _ in range(NTB)]
                    Vb = Buf()
                    e4 = f32r
                    f2 = Ring(nc, esa, f"f2_{l}_", [128, 512], F32, 2)
                    l4 = bfr
                    a2 = Ring(nc, esa, f"a2_{l}_", [128, 512], BF16, 2)
                    cbcb = [Buf(), Buf()]
                    r3 = [sba(f"r3_{i}", [3, 512], BF16) for i in range(2)]
                    r3b = [Buf(), Buf()]
                    NN = NT * NH
                    nlf = sba("nlf", [128, NN], F32)
                    cneg = sba("cneg", [128, NN], F32)
                    Cblk = sba("Cblk", [128, NN], F32)
                    Wt_ = sba("Wtri", [128, NN], F32)
                    xf = sba("xf", [128, NN], F32)
                    wfs = sba("wfs", [128, KC * NH], BF16)
                    tmpT = [sba(f"tmpT{i}", [128, 128], F32) for i in range(2)]
                    tmpTb = [Buf(), Buf()]
                    fb_ = Buf()
                    wfsem = DmaSem(tr, "wf")
                    wfb = Buf()

                    def load_V(blk):
                        wV, wVb = wload(win_d[l, blk])
                        for n0 in range(0, NT, 4):
                            pt, pb = bank()
                            for j in range(4):
                                n = n0 + j
                                for k in range(KC):
                                    mm(pt[:, j * 128:(j + 1) * 128], hnT[:, k, n * 128:(n + 1) * 128], wV[:, k * 128:(k + 1) * 128],
                                       k == 0, k == KC - 1, [wVb, hnb[n // 4]], [pb], signal=(k == KC - 1 and j == 3))
                            tr.op("act", lambda e: e.copy(Vp[:, n0:n0 + 4, :], pt[:].rearrange("p (j t) -> p j t", t=128)), reads=[pb], writes=[Vb])

                    tr.dma("pool", wfs[:], wf_d[l], wfsem, writes=[wfb])
                    pt, pb = bank()
                    for n in range(NT):
                        for k in range(KC):
                            mm(pt[:, n * NH:(n + 1) * NH], hnT[:, k, n * 128:(n + 1) * 128], wfs[:, k * NH:(k + 1) * NH], k == 0, k == KC - 1,
                               [wfb, hnb[n // 4]], [pb], signal=(k == KC - 1 and n == NT - 1))
                    fbc = c.P_FB + l * NN
                    tr.op("dve", lambda e: e.tensor_tensor(xf[:], pt[:, 0:NN], prm[:, fbc:fbc + NN], ALU.add), reads=[pb] + CB, writes=[fb_])
                    tr.op("act", lambda e: e.activation(xf[:], xf[:], AF.Exp, scale=-1.0), reads=[fb_], writes=[fb_])
                    tr.op("act", lambda e: e.activation(nlf[:], xf[:], AF.Ln, bias=epsc[:, 1:2]), reads=[fb_], writes=[fb_])
                    pt1, pb1 = bank()
                    mm(pt1[:, 0:NN], ONESF, nlf[:], True, True, [fb_] + CB, [pb1])
                    pt2, pb2 = bank()
                    mm(pt2[:, 0:NN], TRIU, nlf[:], True, True, [fb_] + CB, [pb2])
                    tr.op("dve", lambda e: e.tensor_copy(Cblk[:], pt1[:, 0:NN]), reads=[pb1], writes=[fb_])
                    for n in range(1, NT):
                        tr.op("dve", lambda e, n=n: e.tensor_tensor(Cblk[:, n * NH:(n + 1) * NH], Cblk[:, n * NH:(n + 1) * NH], Cblk[:, (n - 1) * NH:n * NH], ALU.add),
                              reads=[fb_], writes=[fb_])
                    tr.op("dve", lambda e: e.tensor_copy(cneg[:, 0:NH], pt2[:, 0:NH]), reads=[pb2, fb_], writes=[fb_])
                    if NT > 1:
                        tr.op("dve", lambda e: e.tensor_tensor(cneg[:, NH:NN], pt2[:, NH:NN], Cblk[:, 0:NN - NH], ALU.add), reads=[pb2, fb_], writes=[fb_])

                    for pc in range(NP):
                        load_V(c.OFV + pc)
                        def build_cbc(qi):
                            for hh in range(2):
                                h = 2 * pc + hh
                                pt, pb = bank(range(6))
                                for j in range(4):
                                    n = 4 * qi + j
                                    tt, ttb = tmpT[j % 2], tmpTb[j % 2]
                                    tr.op("dve", lambda e: e.tensor_scalar_mul(tt[:], TRIU, nlf[:, n * NH + h:n * NH + h + 1]), reads=[fb_] + CB, writes=[ttb])
                                    mm(pt[:, j * 128:(j + 1) * 128], ONESF, tt[:], True, True, [ttb] + CB, [pb], signal=True)
                                n3, n3b = f32r.get()
                                for j in range(4):
                                    n = 4 * qi + j
                                    if n == 0:
                                        tr.op("dve", lambda e: e.tensor_scalar_mul(n3[0:3, 0:128], pt[0:3, 0:128], -1.0), reads=[pb], writes=[n3b])
                                    else:
                                        col = (n - 1) * NH + h
                                        tr.op("dve", lambda e: e.tensor_scalar(n3[0:3, j * 128:(j + 1) * 128], pt[0:3, j * 128:(j + 1) * 128], Cblk[0:3, col:col + 1], -1.0, ALU.add, ALU.mult),
                                              reads=[pb, fb_], writes=[n3b])
                                pcs = []
                                for i3 in range(3):
                                    pc_, pcb_ = bfr.get()
                                    tr.op("dve", lambda e: e.tensor_copy(pc_[0:3, :], n3[0:3, :]), reads=[n3b], writes=[pcb_])
                                    if i3 < 2:
                                        tr.op("dve", lambda e: e.tensor_tensor(n3[0:3, :], n3[0:3, :], pc_[0:3, :], ALU.subtract), reads=[n3b, pcb_], writes=[n3b])
                                    pcs.append((pc_, pcb_))
                                tr.op("dve", lambda e: e.tensor_scalar_mul(r3[hh][0:3, :], pcs[0][0][0:3, :], IDENT[0:3, 0:1]), reads=[pcs[0][1]] + CB, writes=[r3b[hh]])
                                for i3 in (1, 2):
                                    tr.op("dve", lambda e: e.scalar_tensor_tensor(r3[hh][0:3, :], pcs[i3][0][0:3, :], IDENT[0:3, i3:i3 + 1], r3[hh][0:3, :], ALU.mult, ALU.add),
                                          reads=[pcs[i3][1], r3b[hh]] + CB, writes=[r3b[hh]])
                        wQ, wQb = wload(win_d[l, c.OFQ + pc])
                        wK, wKb = wload(win_d[l, c.OFK + pc])
                        for (w_, wb__, dstT, dstb, gcol) in ((wQ, wQb, qT, qb_, gqs[:, l:l + 1]), (wK, wKb, kT, kb_, prm[:, c.P_GK + l:c.P_GK + l + 1])):
                            for tb in range(NTB):
                                tc_ = slice(tb * 512, (tb + 1) * 512)
                                pQ, pQb = proj(w_, wb__, hnT, [hnb[tb]], tc_, KC)
                                sq, sqb = bfr.get()
                                tr.op("act", lambda e: e.activation(sq[:], pQ, AF.Square), reads=[pQb], writes=[sqb])
                                pt, pb = bank()
                                mm(pt[:], BD, sq[:], True, True, [sqb] + CB, [pb])
                                rs, rsb = f32r.get()
                                tr.op("act", lambda e: e.activation(rs[:], pt[:], AF.Sqrt, bias=epsc[:, 0:1], scale=1.0 / 64), reads=[pb] + CB, writes=[rsb])
                                tr.op("dve", lambda e: e.reciprocal(rs[:], rs[:]), reads=[rsb], writes=[rsb])
                                tr.op("dve", lambda e: e.scalar_tensor_tensor(dstT[:, tc_], pQ, gcol, rs[:], ALU.mult, ALU.mult), reads=[pQb, rsb] + CB, writes=[dstb[tb]])
                        for qi in range(NTB):
                            q0 = qi * 512
                            build_cbc(qi)
                            pO, pOb = ps[6], psb[6]
                            pD, pDb = ps[7], psb[7]
                            nkb = 4 * qi + 4
                            st = {}

                            def stageA(kb):
                                r = kb - 4 * qi
                                c0 = 128 * r if r > 0 else 0
                                diag = r >= 0
                                for hh in range(2):
                                    h = 2 * pc + hh
                                    hs = slice(hh * 64, hh * 64 + 64)
                                    pZ, pZb = bank(range(6))
                                    mm(pZ[:, c0:512], kT[hs, kb * 128:(kb + 1) * 128], qT[hs, q0 + c0:q0 + 512], True, False, [kb_[kb // 4], qb_[qi]], [pZb], signal=False, skip=True)
                                    mm(pZ[:, c0:512], ONES[0:3, 0:128], r3[hh][0:3, c0:512], False, not diag, [r3b[hh]] + CB, [pZb], signal=not diag, skip=True)
                                    if diag:
                                        mm(pZ[:, c0:c0 + 128], IDENTB, NEGM, False, True, CB, [pZb], signal=True, skip=True)
                                    P, Pb = l4.get()
                                    col = kb * NH + h
                                    tr.op("act", lambda e: e.activation(P[:, c0:512], pZ[:, c0:512], AF.Exp, bias=cneg[:, col:col + 1]), reads=[pZb, fb_], writes=[Pb])
                                    st[(kb, hh)] = (P, Pb, c0)

                            def stageB(kb):
                                for hh in range(2):
                                    P, Pb, c0 = st.pop((kb, hh))
                                    hs = slice(hh * 64, hh * 64 + 64)
                                    mm(pO[hs, c0:512], Vp[:, kb, hs], P[:, c0:512], kb == 0, kb == nkb - 1, [Vb, Pb], [pOb], signal=False)
                                    mm(pD[hs, c0:512], ONES[:, 0:64], P[:, c0:512], kb == 0, kb == nkb - 1, [Pb] + CB, [pDb], signal=True)
                            stageA(0)
                            stageA(1)
                            for kb in range(nkb):
                                if kb + 2 < nkb:
                                    stageA(kb + 2)
                                stageB(kb)
                            rd, rdb = f2.get()
                            tr.op("dve", lambda e: e.reciprocal(rd[:], pD[:]), reads=[pDb], writes=[rdb])
                            tr.op("dve", lambda e: e.tensor_tensor(foxT[:, pc, q0:q0 + 512], pO[:], rd[:], ALU.mult), reads=[pOb, rdb], writes=[foxb[pc][qi]])

                    for pc in range(NP):
                        load_V(c.OSV + pc)
                        wQ, wQb = wload(win_d[l, c.OSQ + pc])
                        wK, wKb = wload(win_d[l, c.OSK + pc])
                        for tb in range(NTB):
                            tc_ = slice(tb * 512, (tb + 1) * 512)
                            pQ, pQb = proj(wQ, wQb, hnT, [hnb[tb]], tc_, KC)
                            tr.op("act", lambda e: e.activation(qT[:, tc_], pQ, AF.Copy, scale=0.125), reads=[pQb], writes=[qb_[tb]])
                            pK, pKb = proj(wK, wKb, hnT, [hnb[tb]], tc_, KC)
                            tr.op("dve", lambda e: e.tensor_copy(kT[:, tc_], pK), reads=[pKb], writes=[kb_[tb]])
                        for qi in range(NTB):
                            q0 = qi * 512
                            pO, pOb = ps[6], psb[6]
                            pX = [ps[4], ps[5]]
                            pXb = [psb[4], psb[5]]
                            nkb = 4 * qi + 4
                            for hh in range(2):
                                mm(pX[hh][:], ZEROS, hnT[:, 0, 0:512], True, True, [hnb[0]] + CB, [pXb[hh]], signal=False, skip=True)
                            mm(pO[:], ZEROS, hnT[:, 0, 0:512], True, True, [hnb[0]] + CB, [pOb], signal=False, skip=True)
                            st = {}

                            def sA(kb):
                                r = kb - 4 * qi
                                c0 = 128 * r if r > 0 else 0
                                for hh in range(2):
                                    hs = slice(hh * 64, hh * 64 + 64)
                                    pZ, pZb = bank(range(4))
                                    mm(pZ[:, c0:512], kT[hs, kb * 128:(kb + 1) * 128], qT[hs, q0 + c0:q0 + 512], True, True, [kb_[kb // 4], qb_[qi]], [pZb])
                                    E, Eb = e4.get()
                                    tr.op("act", lambda e: e.activation(E[:, c0:512], pZ[:, c0:512], AF.Exp), reads=[pZb], writes=[Eb])
                                    if r >= 0:
                                        tr.op("dve", lambda e: e.tensor_tensor(E[:, c0:c0 + 128], E[:, c0:c0 + 128], MLT, ALU.mult), reads=[Eb] + CB, writes=[Eb])
                                    Lp, Lpb = l4.get()
                                    tr.op("act", lambda e: e.activation(Lp[:, c0:512], E[:, c0:512], AF.Ln, bias=epsc[:, 1:2]), reads=[Eb], writes=[Lpb])
                                    st[(kb, hh)] = (E, Eb, Lp, Lpb, c0)

                            def sB1(kb):
                                for hh in range(2):
                                    E, Eb, Lp, Lpb, c0 = st[(kb, hh)]
                                    mm(pX[hh][:, c0:512], NTI, Lp[:, c0:512], False, True, [Lpb] + CB, [pXb[hh]], skip=True)
                                fs = []
                                for hh in range(2):
                                    E, Eb, Lp, Lpb, c0 = st[(kb, hh)]
                                    Fx, Fb = f2.get()
                                    tr.op("act", lambda e: e.activation(Fx[:, c0:512], pX[hh][:, c0:512], AF.Exp), reads=[pXb[hh]], writes=[Fb])
                                    fs.append((Fx, Fb))
                                return fs

                            def sB2(kb, fs):
                                for hh in range(2):
                                    E, Eb, Lp, Lpb, c0 = st.pop((kb, hh))
                                    Fx, Fb = fs[hh]
                                    hs = slice(hh * 64, hh * 64 + 64)
                                    mm(pX[hh][:, c0:512], NTS, Lp[:, c0:512], False, True, [Lpb] + CB, [pXb[hh]], signal=False, skip=True)
                                    A, Ab = a2.get()
                                    tr.op("dve", lambda e: e.tensor_tensor(A[:, c0:512], E[:, c0:512], Fx[:, c0:512], ALU.mult), reads=[Eb, Fb], writes=[Ab])
                                    mm(pO[hs, c0:512], Vp[:, kb, hs], A[:, c0:512], False, True, [Vb, Ab], [pOb], skip=True)
                            order = list(range(nkb - 1, -1, -1))
                            sA(order[0])
                            sA(order[1])
                            for i_, kb in enumerate(order):
                                fs = sB1(kb)
                                if i_ + 2 < len(order):
                                    sA(order[i_ + 2])
                                sB2(kb, fs)
                            tr.op("act", lambda e: e.copy(sbT[:, pc, q0:q0 + 512], pO[:]), reads=[pOb], writes=[sbb[pc][qi]])

                tr.fence()
                wring[0] = list(zip(wsl, wslb, wsem))
                with ExitStack() as esg:
                    MG = min(S, 1024)
                    NMB = MG // 512
                    mT = esg.enter_context(nc.sbuf_tensor(f"mT_{l}", [128, KC, MG], BF16))
                    macc = esg.enter_context(nc.sbuf_tensor(f"macc_{l}", [128, MG], F32))
                    mb = [Buf() for _ in range(NMB)]
                    maccb = [Buf() for _ in range(NMB)]
                    extra_slots(esg, f"m{l}")
                    brs = ((convT, convb, wpc_d, NCC), (foxT, foxb, wpf_d, NP), (sbT, sbb, wps_d, NP))
                    for mg in range(S // MG):
                        for oc in range(KC):
                            for br in range(3):
                                srcT, srcb, wd_, nx = brs[br]
                                wp_, wpb_ = wload(wd_[l, oc])
                                wg_, wgb_ = wload(win_d[l, c.OG + br * KC + oc])
                                gcol = c.P_GB + l * 3 * KC + br * KC + oc
                                for j in range(NMB):
                                    tb = mg * NMB + j
                                    tc_ = slice(tb * 512, (tb + 1) * 512)
                                    lc_ = slice(j * 512, (j + 1) * 512)
                                    pY, pYb = proj(wp_, wpb_, srcT, [srcb[k][tb] for k in range(nx)], tc_, nx)
                                    pG, pGb = proj(wg_, wgb_, hnT, [hnb[tb]], tc_, KC)
                                    g, gb = f32r.get()
                                    tr.op("act", lambda e: e.activation(g[:], pG, AF.Sigmoid, bias=prm[:, gcol:gcol + 1]), reads=[pGb] + CB, writes=[gb])
                                    if br == 0:
                                        tr.op("dve", lambda e: e.tensor_tensor(macc[:, lc_], g[:], pY, ALU.mult), reads=[gb, pYb], writes=[maccb[j]])
                                    else:
                                        tr.op("dve", lambda e: e.tensor_tensor(g[:], g[:], pY, ALU.mult), reads=[gb, pYb], writes=[gb])
                                        if br == 1:
                                            tr.op("dve", lambda e: e.tensor_tensor(macc[:, lc_], macc[:, lc_], g[:], ALU.add), reads=[gb, maccb[j]], writes=[maccb[j]])
                                        else:
                                            tr.op("dve", lambda e: e.tensor_tensor(mT[:, oc, lc_], macc[:, lc_], g[:], ALU.add), reads=[gb, maccb[j]], writes=[mb[j]])
                        for oc in range(KC):
                            wo_, wob_ = wload(wout_d[l, oc])
                            for j in range(NMB):
                                tb = mg * NMB + j
                                tc_ = slice(tb * 512, (tb + 1) * 512)
                                lc_ = slice(j * 512, (j + 1) * 512)
                                pR, pRb = proj(wo_, wob_, mT, [mb[j]], lc_, KC)
                                tr.op("dve", lambda e: e.tensor_tensor(xT[:, oc, tc_], xT[:, oc, tc_], pR, ALU.add), reads=[pRb, xTb[oc][tb]], writes=[xTb[oc][tb]])

            tr.fence()
            wring[0] = list(zip(wsl, wslb, wsem))
            norm(c.P_G2 + l * KC)
            with ExitStack() as esf:
                TG, NG = c.TG, c.NG
                NTG = TG // 512
                actT = esf.enter_context(nc.sbuf_tensor(f"actT_{l}", [128, NFF, TG], BF16))
                ug = esf.enter_context(nc.sbuf_tensor(f"ug_{l}", [128, 2 + TG], F32))
                actb = [[Buf() for _ in range(NTG)] for _ in range(NFF)]
                ugb = [Buf() for _ in range(NTG)]
                ug0b = Buf()
                extra_slots(esf, f"f{l}")
                for gi in range(NG):
                    g0 = gi * TG
                    for cf in range(NFF):
                        wG, wGb = wload(wup_d[l, cf])
                        wV, wVb = wload(wup_d[l, NFF + cf])
                        fwc = c.P_FW + (l * NFF + cf) * 3
                        fcc = c.P_FC + l * NFF + cf
                        if gi == 0:
                            tr.op("dve", lambda e: e.memset(ug[:, 0:2], 0.0), writes=[ug0b])
                        else:
                            pt, pb = bank()
                            tbp = (g0 - 2) // 512
                            proj(wG, wGb, hnT, [hnb[tbp]], slice(g0 - 2, g0), KC, out=pt[:, 0:2], outb=pb)
                            tr.op("act", lambda e: e.copy(ug[:, 0:2], pt[:, 0:2]), reads=[pb], writes=[ug0b])
                        for j in range(NTG):
                            tb = gi * NTG + j
                            tc_ = slice(tb * 512, (tb + 1) * 512)
                            o = j * 512
                            pG, pGb = proj(wG, wGb, hnT, [hnb[tb]], tc_, KC)
                            pV, pVb = proj(wV, wVb, hnT, [hnb[tb]], tc_, KC)
                            tr.op("act", lambda e: e.copy(ug[:, 2 + o:2 + o + 512], pG), reads=[pGb], writes=[ugb[j]])
                            prev = [ugb[j - 1]] if j > 0 else [ug0b]
                            t1, t1b = f32r.get()
                            tr.op("dve", lambda e: e.tensor_scalar(t1[:], ug[:, 2 + o:2 + o + 512], prm[:, fwc + 2:fwc + 3], prm[:, fcc:fcc + 1], ALU.mult, ALU.add),
                                  reads=[ugb[j]] + CB, writes=[t1b])
                            tr.op("dve", lambda e: e.scalar_tensor_tensor(t1[:], ug[:, 1 + o:1 + o + 512], prm[:, fwc + 1:fwc + 2], t1[:], ALU.mult, ALU.add),
                                  reads=[ugb[j], t1b] + prev + CB, writes=[t1b])
                            tr.op("dve", lambda e: e.scalar_tensor_tensor(t1[:], ug[:, o:o + 512], prm[:, fwc:fwc + 1], t1[:], ALU.mult, ALU.add),
                                  reads=[ugb[j], t1b] + prev + CB, writes=[t1b])
                            tr.op("act", lambda e: e.activation(t1[:], t1[:], AF.Silu), reads=[t1b], writes=[t1b])
                            tr.op("dve", lambda e: e.tensor_tensor(actT[:, cf, o:o + 512], t1[:], pV, ALU.mult), reads=[t1b, pVb], writes=[actb[cf][j]])
                    for oc in range(KC):
                        halves = []
                        for hk in range(0, NFF, c.HK):
                            n_ = min(c.HK, NFF - hk)
                            wd_, wdb_ = wload(wdn_d[l, oc][:, hk * 128:(hk + n_) * 128])
                            halves.append((hk, n_, wd_, wdb_))
                        for j in range(NTG):
                            tb = gi * NTG + j
                            tc_ = slice(tb * 512, (tb + 1) * 512)
                            pt, pb = bank()
                            for (hk, n_, wd_, wdb_) in halves:
                                for k in range(n_):
                                    cf = hk + k
                                    mm(pt[:], wd_[:, k * 128:(k + 1) * 128], actT[:, cf, j * 512:(j + 1) * 512], cf == 0, cf == NFF - 1,
                                       [wdb_, actb[cf][j]], [pb], signal=(cf == NFF - 1))
                            tr.op("dve", lambda e: e.tensor_tensor(xT[:, oc, tc_], xT[:, oc, tc_], pt[:], ALU.add), reads=[pb, xTb[oc][tb]], writes=[xTb[oc][tb]])

            tr.fence()
            wring[0] = list(zip(wsl, wslb, wsem))

        with ExitStack() as es3:
            yo = [es3.enter_context(nc.sbuf_tensor(f"yo{i}", [128, D], F32)) for i in range(2)]
            yob = [Buf(), Buf()]
            osem = [DmaSem(tr, "out0"), DmaSem(tr, "out1")]
            for n in range(NT):
                k = n % 2
                for k0 in range(0, KC, 4):
                    nk = min(4, KC - k0)
                    pt, pb = bank()
                    for j in range(nk):
                        tr.op("pe", lambda e, j=j: e.transpose(pt[:, j * 128:(j + 1) * 128], xT[:, k0 + j, n * 128:(n + 1) * 128], IDENT),
                              reads=[xTb[k0 + j][n // 4]] + CB, writes=[pb], signal=(j == nk - 1))
                    if (n + k0 // 4) % 2 == 0:
                        tr.op("act", lambda e: e.copy(yo[k][:, k0 * 128:(k0 + nk) * 128], pt[:, 0:nk * 128]), reads=[pb], writes=[yob[k]])
                    else:
                        tr.op("dve", lambda e: e.tensor_copy(yo[k][:, k0 * 128:(k0 + nk) * 128], pt[:, 0:nk * 128]), reads=[pb], writes=[yob[k]])
                tr.dma("sp", y_d[n * 128:(n + 1) * 128, :], yo[k][:], osem[k], reads=[yob[k]])
            for os_ in osem:
                nc.sync.wait_ge(os_.sem, os_.cnt)
        build_nc.stats = (tr.ninst, tr.nsem)
    return nc


_NC_CACHE = {}


def kernel(**inputs):
    cfg = Cfg()
    x = np.asarray(inputs["x"], np.float32)
    B = x.shape[0]
    lay = host_layout(cfg, inputs)
    if "nc" not in _NC_CACHE:
        _NC_CACHE["nc"] = build_nc(cfg)
    nc = _NC_CACHE["nc"]
    in_maps = []
    for b in range(B):
        m = dict(lay)
        m["x"] = np.ascontiguousarray(x[b])
        in_maps.append(m)
    res = run_bass_kernel_spmd(nc, in_maps, core_ids=list(range(B)))
    return np.stack([np.asarray(r["y"], np.float32) for r in res.results], axis=0)
```

```python
import numpy as np
from contextlib import ExitStack
import concourse.bass as bass
import concourse.mybir as mybir
from concourse.bass_utils import run_bass_kernel_spmd

F32 = mybir.dt.float32
BF16 = mybir.dt.bfloat16
AF = mybir.ActivationFunctionType
ALU = mybir.AluOpType
EPS = 1e-6


class Cfg:
    def __init__(self, S=2048, D=1024, DEPTH=4, NP=4, NCC=4, NFF=22, TG=1024):
        self.S, self.D, self.DEPTH, self.NP, self.NCC, self.NFF = S, D, DEPTH, NP, NCC, NFF
        self.KC = D // 128
        self.NT = S // 128
        self.NTB = S // 512
        self.NH = 2 * NP
        self.CW = NCC * 128
        self.FW = NP * 128
        self.DFF = NFF * 128
        self.DIN = 3 * self.CW + 3 * self.FW + self.NH + 3 * self.FW + 3 * D
        self.TG = min(S, TG)
        self.NG = S // self.TG
        self.HK = (NFF + 1) // 2
        self.WSZ = max(self.KC, self.HK, NCC, NP) * 128
        self.OB, self.OC, self.OH = 0, NCC, 2 * NCC
        self.OFQ = 3 * NCC
        self.OFK = self.OFQ + NP
        self.OFV = self.OFK + NP
        self.OSQ = self.OFV + NP
        self.OSK = self.OSQ + NP
        self.OSV = self.OSK + NP
        self.OG = self.OSV + NP
        self.NBLK = self.OG + 3 * self.KC
        o = 0
        self.P_G1 = o; o += DEPTH * self.KC
        self.P_G2 = o; o += DEPTH * self.KC
        self.P_GB = o; o += DEPTH * 3 * self.KC
        self.P_CW = o; o += DEPTH * NCC * 3
        self.P_FB = o; o += DEPTH * self.NT * self.NH
        self.P_GQ = o; o += DEPTH
        self.P_GK = o; o += DEPTH
        self.P_FW = o; o += DEPTH * NFF * 3
        self.P_FC = o; o += DEPTH * NFF
        self.NPRM = o


def _blocks(W, col_starts):
    K = W.shape[0]
    kc = K // 128
    out = np.empty((len(col_starts), 128, kc * 128), np.float32)
    Wr = W.reshape(kc, 128, W.shape[1])
    for i, c0 in enumerate(col_starts):
        out[i] = Wr[:, :, c0:c0 + 128].transpose(1, 0, 2).reshape(128, kc * 128)
    return out


def host_layout(cfg, inp):
    c = cfg
    L = c.DEPTH
    d = {}
    w_in = np.asarray(inp["w_in"], np.float32)
    fcol = 3 * c.CW + 3 * c.FW
    sb0 = fcol + c.NH
    g0 = sb0 + 3 * c.FW
    starts = []
    for grp in range(3):
        starts += [grp * c.CW + i * 128 for i in range(c.NCC)]
    for grp in range(3):
        starts += [3 * c.CW + grp * c.FW + i * 128 for i in range(c.NP)]
    for grp in range(3):
        starts += [sb0 + grp * c.FW + i * 128 for i in range(c.NP)]
    for br in range(3):
        starts += [g0 + br * c.D + i * 128 for i in range(c.KC)]
    assert len(starts) == c.NBLK
    d["win"] = np.stack([_blocks(w_in[l], starts) for l in range(L)])
    wf = w_in[:, :, fcol:fcol + c.NH].reshape(L, c.KC, 128, c.NH).transpose(0, 2, 1, 3)
    d["wf"] = np.ascontiguousarray(wf).reshape(L, 128, c.KC * c.NH)
    oc_starts = [i * 128 for i in range(c.KC)]
    d["wpc"] = np.stack([_blocks(np.asarray(inp["w_proj_conv"][l], np.float32), oc_starts) for l in range(L)])
    d["wpf"] = np.stack([_blocks(np.asarray(inp["w_proj_fox"][l], np.float32), oc_starts) for l in range(L)])
    d["wps"] = np.stack([_blocks(np.asarray(inp["w_proj_sb"][l], np.float32), oc_starts) for l in range(L)])
    d["wout"] = np.stack([_blocks(np.asarray(inp["w_out"][l], np.float32), oc_starts) for l in range(L)])
    d["wup"] = np.stack([_blocks(np.asarray(inp["w_up"][l], np.float32), [i * 128 for i in range(2 * c.NFF)]) for l in range(L)])
    d["wdn"] = np.stack([_blocks(np.asarray(inp["w_down"][l], np.float32), oc_starts) for l in range(L)])
    prm = np.zeros((128, c.NPRM), np.float32)

    def pm(v, n):
        return np.asarray(v, np.float32).reshape(L, n, 128).transpose(2, 0, 1).reshape(128, L * n)
    prm[:, c.P_G1:c.P_G1 + L * c.KC] = pm(inp["norm1_g"], c.KC)
    prm[:, c.P_G2:c.P_G2 + L * c.KC] = pm(inp["norm2_g"], c.KC)
    prm[:, c.P_GB:c.P_GB + L * 3 * c.KC] = pm(inp["gate_bias"], 3 * c.KC)
    cw = np.asarray(inp["conv_w"], np.float32).reshape(L, 3, c.NCC, 128).transpose(3, 0, 2, 1)
    prm[:, c.P_CW:c.P_CW + L * c.NCC * 3] = cw.reshape(128, -1)
    fb = np.asarray(inp["fox_f_bias"], np.float32)
    prm[:, c.P_FB:c.P_FB + L * c.NT * c.NH] = np.broadcast_to(fb[None, :, None, :], (128, L, c.NT, c.NH)).reshape(128, -1)
    gq = np.asarray(inp["fox_q_norm_g"], np.float32)
    gk = np.asarray(inp["fox_k_norm_g"], np.float32)
    prm[:, c.P_GQ:c.P_GQ + L] = np.concatenate([gq, gq], axis=1).T
    prm[:, c.P_GK:c.P_GK + L] = np.concatenate([gk, gk], axis=1).T
    fw = np.asarray(inp["ffn_conv_w"], np.float32).reshape(L, 3, c.NFF, 128).transpose(3, 0, 2, 1)
    prm[:, c.P_FW:c.P_FW + L * c.NFF * 3] = fw.reshape(128, -1)
    prm[:, c.P_FC:c.P_FC + L * c.NFF] = pm(inp["ffn_conv_b"], c.NFF)
    d["prm"] = prm
    i = np.arange(128)
    le = (i[:, None] <= i[None, :]).astype(np.float32)
    lt = (i[:, None] < i[None, :]).astype(np.float32)
    cst = np.zeros((128, 12 * 128), np.float32)
    cst[:, 0:128] = np.eye(128)
    cst[:, 128:256] = le
    cst[:, 256:384] = 1.0
    cst[:, 384:512] = np.where(le > 0, 1e30, -1e4)
    cst[:, 512:640] = lt
    cst[:, 640:768] = 1.0
    bd = np.zeros((128, 128), np.float32); bd[:64, :64] = 1; bd[64:, 64:] = 1
    cst[:, 768:896] = bd
    cst[:, 896:1024] = -(i[:, None] >= i[None, :]).astype(np.float32)
    cst[:, 1024:1152] = -(i[:, None] < i[None, :]).astype(np.float32)
    cst[:, 1152:1280] = 0.0
    cst[:, 1280:1408] = np.eye(128)
    cst[:, 1408:1536] = np.where(le > 0, 0.0, -1e4)
    d["cst"] = cst
    return d


class Buf:
    __slots__ = ("w", "r", "name")

    def __init__(self, name=""):
        self.w = []
        self.r = []
        self.name = name


class Tracker:
    ROT = 2000

    def __init__(self, nc, es):
        self.nc, self.es = nc, es
        self.E = {"pe": nc.tensor, "act": nc.scalar, "dve": nc.vector, "pool": nc.gpsimd, "sp": nc.sync}
        self.sem, self.cnt, self.gen, self.key = {}, {}, {}, {}
        self.seen = {k: {} for k in self.E}
        self.nsem = 0
        self.ninst = 0
        self.dsems = []
        self.prev = {}
        for k in self.E:
            self._newsem(k)

    def _newsem(self, k):
        g = self.gen.get(k, -1) + 1
        if g > 0:
            self.prev[k] = (self.key[k], self.sem[k], self.cnt[k])
        self.gen[k] = g
        self.sem[k] = self.es.enter_context(self.nc.semaphore(f"s_{k}_{g}"))
        self.cnt[k] = 0
        self.key[k] = f"{k}:{g}"
        self.nsem += 1

    def _waits(self, e, reads, writes):
        need = {}

        def add(tok, raw):
            key, sem, val = tok
            own = key.split(":")[0] == e
            if own and e == "pe":
                return
            if need.get(key, (None, 0))[1] < val:
                need[key] = (sem, val)
        for b in reads:
            for t in b.w:
                add(t, True)
        for b in writes:
            for t in b.w:
                add(t, True)
            for t in b.r:
                add(t, False)
        eng = self.E[e]
        for key, (sem, val) in need.items():
            if self.seen[e].get(key, 0) >= val:
                continue
            eng.wait_ge(sem, val)
            self.seen[e][key] = val

    @staticmethod
    def _addr(b, tok):
        for i, t in enumerate(b.r):
            if t[0] == tok[0]:
                if t[2] < tok[2]:
                    b.r[i] = tok
                return
        b.r.append(tok)

    def op(self, e, fn, reads=(), writes=(), signal=True):
        self._waits(e, reads, writes)
        ins = fn(self.E[e])
        self.ninst += 1
        if signal:
            self.cnt[e] += 1
            ins.then_inc(self.sem[e], 1)
            tok = (self.key[e], self.sem[e], self.cnt[e])
        else:
            tok = (self.key[e], self.sem[e], self.cnt[e] + 1)
        for b in reads:
            self._addr(b, tok)
        for b in writes:
            b.w = [tok]
            b.r = []
        if signal and self.cnt[e] >= self.ROT:
            self._newsem(e)
        return tok

    def fence(self):
        for e, eng in self.E.items():
            toks = []
            for f in self.E:
                if f == e:
                    continue
                if self.cnt[f] > 0:
                    toks.append((self.key[f], self.sem[f], self.cnt[f]))
                elif f in self.prev:
                    toks.append(self.prev[f])
            for d in self.dsems:
                if d.cnt > 0:
                    toks.append((d.key, d.sem, d.cnt))
                elif d.prev is not None:
                    toks.append(d.prev)
            for key, sem, val in toks:
                if self.seen[e].get(key, 0) < val:
                    eng.wait_ge(sem, val)
                    self.seen[e][key] = val

    def dma(self, e, out, in_, dsem, reads=(), writes=()):
        self._waits(e, reads, writes)
        if dsem.cnt >= 1600:
            dsem.rotate()
        ins = self.E[e].dma_start(out=out, in_=in_)
        ins.then_inc(dsem.sem, 16)
        dsem.cnt += 16
        self.ninst += 1
        tok = (dsem.key, dsem.sem, dsem.cnt)
        for b in reads:
            self._addr(b, tok)
        for b in writes:
            b.w = [tok]
            b.r = []
        return tok


class DmaSem:
    _n = 0

    def __init__(self, tr, name):
        self.tr, self.name, self.prev = tr, name, None
        self._new()
        tr.dsems.append(self)

    def _new(self):
        DmaSem._n += 1
        self.sem = self.tr.es.enter_context(self.tr.nc.semaphore(f"d_{self.name}_{DmaSem._n}"))
        self.cnt = 0
        self.key = f"dma{DmaSem._n}:{self.name}"

    def rotate(self):
        self.prev = (self.key, self.sem, self.cnt)
        self._new()


class Ring:
    def __init__(self, nc, es, name, shape, dtype, n):
        self.t = [es.enter_context(nc.sbuf_tensor(f"{name}{i}", shape, dtype)) for i in range(n)]
        self.b = [Buf(f"{name}{i}") for i in range(n)]
        self.i = 0

    def get(self):
        k = self.i % len(self.t)
        self.i += 1
        return self.t[k], self.b[k]


def build_nc(cfg):
    c = cfg
    S, D, L, KC, NT, NTB, NH, NP, NCC, NFF = c.S, c.D, c.DEPTH, c.KC, c.NT, c.NTB, c.NH, c.NP, c.NCC, c.NFF
    nc = bass.Bass("TRN2", target_bir_lowering=False)
    dr = lambda n, sh, kind="ExternalInput": nc.dram_tensor(n, sh, F32, kind=kind).ap()
    x_d = dr("x", [S, D])
    win_d = dr("win", [L, c.NBLK, 128, KC * 128])
    wf_d = dr("wf", [L, 128, KC * NH])
    wpc_d = dr("wpc", [L, KC, 128, NCC * 128])
    wpf_d = dr("wpf", [L, KC, 128, NP * 128])
    wps_d = dr("wps", [L, KC, 128, NP * 128])
    wout_d = dr("wout", [L, KC, 128, KC * 128])
    wup_d = dr("wup", [L, 2 * NFF, 128, KC * 128])
    wdn_d = dr("wdn", [L, KC, 128, NFF * 128])
    prm_d = dr("prm", [128, c.NPRM])
    cst_d = dr("cst", [128, 1536])
    y_d = dr("y", [S, D], "ExternalOutput")

    with ExitStack() as es:
        tr = Tracker(nc, es)
        sb = lambda n, sh, dt: es.enter_context(nc.sbuf_tensor("sb_" + n, sh, dt))
        xT = sb("xT", [128, KC, S], F32)
        hnT = sb("hnT", [128, KC, S], BF16)
        xTb = [[Buf(f"xT{k}_{t}") for t in range(NTB)] for k in range(KC)]
        hnb = [Buf(f"hn{t}") for t in range(NTB)]
        prm = sb("prm", [128, c.NPRM], F32)
        cstf = sb("cstf", [128, 640], F32)
        cstb = sb("cstb", [128, 896], BF16)
        cst_b = Buf("cst")
        IDENT, TRIU, ONESF, CAP, MLT = (cstf[:, i * 128:(i + 1) * 128] for i in range(5))
        ONES, BD, NTI, NTS, ZEROS, IDENTB, NEGM = (cstb[:, i * 128:(i + 1) * 128] for i in range(7))
        ps = [es.enter_context(nc.psum_tensor(f"ps{i}", [128, 512], F32)) for i in range(8)]
        psb = [Buf(f"ps{i}") for i in range(8)]
        psi = [0]

        def bank(allowed=range(8)):
            allowed = list(allowed)
            k = allowed[psi[0] % len(allowed)]
            psi[0] += 1
            return ps[k], psb[k]
        NW = 4
        wsl = [sb(f"wsl{i}", [128, c.WSZ], BF16) for i in range(NW)]
        wslb = [Buf(f"wsl{i}") for i in range(NW)]
        wsem = [DmaSem(tr, f"w{i}") for i in range(NW)]
        wi = [0]

        wring = [list(zip(wsl, wslb, wsem))]

        def wload(src):
            ring = wring[0]
            k = wi[0] % len(ring)
            wi[0] += 1
            n = src.shape[1]
            t_, b_, s_ = ring[k]
            tr.dma("pool", t_[:, 0:n], src, s_, writes=[b_])
            return t_, b_

        def extra_slots(stack, tag, keep=1536):
            n_ = max(0, min(12, (nc.sbuf_bytes_remaining - keep) // (c.WSZ * 2)))
            ex = []
            for i in range(n_):
                t_ = stack.enter_context(nc.sbuf_tensor(f"wx_{tag}_{i}", [128, c.WSZ], BF16))
                ex.append((t_, Buf(), DmaSem(tr, f"wx{i}")))
            wring[0] = list(zip(wsl, wslb, wsem)) + ex
        f32r = Ring(nc, es, "f32r", [128, 512], F32, 6)
        bfr = Ring(nc, es, "bfr", [128, 512], BF16, 6)

        def mm(out, lhsT, rhs, start, stop, reads, writes, signal=True, skip=False):
            if skip:
                return tr.op("pe", lambda e: e.matmul(out, lhsT=lhsT, rhs=rhs, start=start, stop=stop, skip_group_check=True), reads, writes, signal)
            return tr.op("pe", lambda e: e.matmul(out, lhsT=lhsT, rhs=rhs, start=start, stop=stop), reads, writes, signal)

        def proj(w, wb, srcT, srcb, tcols, nk, out=None, outb=None):
            if out is None:
                pt, pb = bank()
                out = pt[:, 0:tcols.stop - tcols.start]
                outb = pb
            for k in range(nk):
                mm(out, w[:, k * 128:(k + 1) * 128], srcT[:, k, tcols], k == 0, k == nk - 1, [wb] + list(srcb), [outb], signal=(k == nk - 1))
            return out, outb

        csem = DmaSem(tr, "cst")
        tr.dma("sp", prm[:], prm_d, csem, writes=[cst_b])
        tr.dma("sp", cstf[:], cst_d[:, 0:640], csem, writes=[cst_b])
        with nc.sbuf_tensor("cb16tmp", [128, 896], F32) as cb16:
            tr.dma("sp", cb16[:], cst_d[:, 640:1536], csem, writes=[cst_b])
            tr.op("dve", lambda e: e.tensor_copy(cstb[:], cb16[:]), reads=[cst_b], writes=[cst_b])
        tr.fence()
        gqs = sb("gqs", [128, L], F32)
        tr.op("dve", lambda e: e.tensor_scalar_mul(gqs[:], prm[:, c.P_GQ:c.P_GQ + L], 0.125), reads=[cst_b], writes=[cst_b])
        epsc = sb("epsc", [128, 2], F32)
        tr.op("dve", lambda e: e.memset(epsc[:, 0:1], EPS), writes=[cst_b])
        tr.op("dve", lambda e: e.memset(epsc[:, 1:2], 1.0), writes=[cst_b])
        CB = [cst_b]

        with ExitStack() as es2:
            xin = [es2.enter_context(nc.sbuf_tensor(f"xin{i}", [128, D], F32)) for i in range(2)]
            xinb = [Buf(), Buf()]
            xsem = [DmaSem(tr, "xin0"), DmaSem(tr, "xin1")]
            for n in range(NT):
                k = n % 2
                tr.dma("sp", xin[k][:], x_d[n * 128:(n + 1) * 128, :], xsem[k], writes=[xinb[k]])
                for k0 in range(0, KC, 4):
                    nk = min(4, KC - k0)
                    pt, pb = bank()
                    for j in range(nk):
                        tr.op("pe", lambda e, j=j: e.transpose(pt[:, j * 128:(j + 1) * 128], xin[k][:, (k0 + j) * 128:(k0 + j + 1) * 128], IDENT),
                              reads=[xinb[k]] + CB, writes=[pb], signal=(j == nk - 1))
                    dst = xT[:, k0:k0 + nk, n * 128:(n + 1) * 128]
                    src = pt[:, 0:nk * 128].rearrange("p (j t) -> p j t", t=128)
                    wb_ = [xTb[k0 + j][n // 4] for j in range(nk)]
                    eng = "act" if (n + k0 // 4) % 2 == 0 else "dve"
                    if eng == "act":
                        tr.op("act", lambda e: e.copy(dst, src), reads=[pb], writes=wb_)
                    else:
                        tr.op("dve", lambda e: e.tensor_copy(dst, src), reads=[pb], writes=wb_)

        tr.fence()

        def norm(gcol0):
            for tb in range(NTB):
                tc_ = slice(tb * 512, (tb + 1) * 512)
                pt, pb = bank()
                for k in range(KC):
                    sq, sqb = bfr.get()
                    tr.op("act", lambda e: e.activation(sq[:], xT[:, k, tc_], AF.Square), reads=[xTb[k][tb]], writes=[sqb])
                    mm(pt[:], ONES, sq[:], k == 0, k == KC - 1, [sqb] + CB, [pb], signal=True)
                rs, rsb = f32r.get()
                tr.op("act", lambda e: e.activation(rs[:], pt[:], AF.Sqrt, bias=epsc[:, 0:1], scale=1.0 / D), reads=[pb] + CB, writes=[rsb])
                tr.op("dve", lambda e: e.reciprocal(rs[:], rs[:]), reads=[rsb], writes=[rsb])
                for k in range(KC):
                    tr.op("dve", lambda e: e.scalar_tensor_tensor(hnT[:, k, tc_], xT[:, k, tc_], prm[:, gcol0 + k:gcol0 + k + 1], rs[:], ALU.mult, ALU.mult),
                          reads=[xTb[k][tb], rsb] + CB, writes=[hnb[tb]])

        HN = hnb

        for l in range(L):
            norm(c.P_G1 + l * KC)
            with ExitStack() as esm:
                sbm = lambda n, sh, dt: esm.enter_context(nc.sbuf_tensor(f"{n}_{l}", sh, dt))
                convT = sbm("convT", [128, NCC, S], BF16)
                foxT = sbm("foxT", [128, NP, S], BF16)
                sbT = sbm("sbT", [128, NP, S], BF16)
                convb = [[Buf() for _ in range(NTB)] for _ in range(NCC)]
                foxb = [[Buf() for _ in range(NTB)] for _ in range(NP)]
                sbb = [[Buf() for _ in range(NTB)] for _ in range(NP)]
                with ExitStack() as esc:
                    u = esc.enter_context(nc.sbuf_tensor(f"u_{l}", [128, 2 + S], F32))
                    ub = [Buf() for _ in range(NTB)]
                    u0b = Buf()
                    tr.op("dve", lambda e: e.memset(u[:, 0:2], 0.0), writes=[u0b])
                    for cc in range(NCC):
                        wB, wBb = wload(win_d[l, c.OB + cc])
                        wC, wCb = wload(win_d[l, c.OC + cc])
                        wH, wHb = wload(win_d[l, c.OH + cc])
                        cwc = c.P_CW + (l * NCC + cc) * 3
                        for tb in range(NTB):
                            tc_ = slice(tb * 512, (tb + 1) * 512)
                            pC, pCb = proj(wC, wCb, hnT, [hnb[tb]], tc_, KC)
                            pH, pHb = proj(wH, wHb, hnT, [hnb[tb]], tc_, KC)
                            pB, pBb = proj(wB, wBb, hnT, [hnb[tb]], tc_, KC)
                            cs, csb = f32r.get()
                            tr.op("act", lambda e: e.copy(cs[:], pC), reads=[pCb], writes=[csb])
                            o = tb * 512
                            tr.op("dve", lambda e: e.tensor_tensor(u[:, 2 + o:2 + o + 512], cs[:], pH, ALU.mult), reads=[csb, pHb], writes=[ub[tb]])
                            t1, t1b = f32r.get()
                            prev = [ub[tb - 1]] if tb > 0 else [u0b]
                            tr.op("dve", lambda e: e.tensor_scalar_mul(t1[:], u[:, 2 + o:2 + o + 512], prm[:, cwc + 2:cwc + 3]), reads=[ub[tb]] + CB, writes=[t1b])
                            tr.op("dve", lambda e: e.scalar_tensor_tensor(t1[:], u[:, 1 + o:1 + o + 512], prm[:, cwc + 1:cwc + 2], t1[:], ALU.mult, ALU.add),
                                  reads=[ub[tb], t1b] + prev + CB, writes=[t1b])
                            tr.op("dve", lambda e: e.scalar_tensor_tensor(t1[:], u[:, o:o + 512], prm[:, cwc:cwc + 1], t1[:], ALU.mult, ALU.add),
                                  reads=[ub[tb], t1b] + prev + CB, writes=[t1b])
                            tr.op("dve", lambda e: e.tensor_tensor(convT[:, cc, tc_], t1[:], pB, ALU.mult), reads=[t1b, pBb], writes=[convb[cc][tb]])

                tr.fence()
                with ExitStack() as esa:
                    sba = lambda n, sh, dt: esa.enter_context(nc.sbuf_tensor(f"{n}_{l}", sh, dt))
                    qT = sba("qT", [128, S], BF16)
                    kT = sba("kT", [128, S], BF16)
                    Vp = sba("Vp", [128, NT, 128], BF16)
                    qb_ = [Buf() for _ in range(NTB)]
                    kb_ = [Buf() for _ in range(NTB)]
                    Vb = Buf()
                    e4 = f32r
                    f2 = Ring(nc, esa, f"f2_{l}_", [128, 512], F32, 2)
                    l4 = bfr
                    a2 = Ring(nc, esa, f"a2_{l}_", [128, 512], BF16, 2)
                    cbcb = [Buf(), Buf()]
                    r3 = [sba(f"r3_{i}", [3, 512], BF16) for i in range(2)]
                    r3b = [Buf(), Buf()]
                    NN = NT * NH
                    nlf = sba("nlf", [128, NN], F32)
                    cneg = sba("cneg", [128, NN], F32)
                    Cblk = sba("Cblk", [128, NN], F32)
                    Wt_ = sba("Wtri", [128, NN], F32)
                    xf = sba("xf", [128, NN], F32)
                    wfs = sba("wfs", [128, KC * NH], BF16)
                    ngc = sba("ngc", [128, NN], F32)
                    PC = sba("PC", [128, NN, 3], BF16)
                    fb_ = Buf()
                    wfsem = DmaSem(tr, "wf")
                    wfb = Buf()

                    def load_V(blk):
                        wV, wVb = wload(win_d[l, blk])
                        for n0 in range(0, NT, 4):
                            pt, pb = bank()
                            for j in range(4):
                                n = n0 + j
                                for k in range(KC):
                                    mm(pt[:, j * 128:(j + 1) * 128], hnT[:, k, n * 128:(n + 1) * 128], wV[:, k * 128:(k + 1) * 128],
                                       k == 0, k == KC - 1, [wVb, hnb[n // 4]], [pb], signal=(k == KC - 1 and j == 3))
                            tr.op("act", lambda e: e.copy(Vp[:, n0:n0 + 4, :], pt[:].rearrange("p (j t) -> p j t", t=128)), reads=[pb], writes=[Vb])

                    tr.dma("pool", wfs[:], wf_d[l], wfsem, writes=[wfb])
                    pt, pb = bank()
                    for n in range(NT):
                        for k in range(KC):
                            mm(pt[:, n * NH:(n + 1) * NH], hnT[:, k, n * 128:(n + 1) * 128], wfs[:, k * NH:(k + 1) * NH], k == 0, k == KC - 1,
                               [wfb, hnb[n // 4]], [pb], signal=(k == KC - 1 and n == NT - 1))
                    fbc = c.P_FB + l * NN
                    tr.op("dve", lambda e: e.tensor_tensor(xf[:], pt[:, 0:NN], prm[:, fbc:fbc + NN], ALU.add), reads=[pb] + CB, writes=[fb_])
                    tr.op("act", lambda e: e.activation(xf[:], xf[:], AF.Exp, scale=-1.0), reads=[fb_], writes=[fb_])
                    tr.op("act", lambda e: e.activation(nlf[:], xf[:], AF.Ln, bias=epsc[:, 1:2]), reads=[fb_], writes=[fb_])
                    pt1, pb1 = bank()
                    mm(pt1[:, 0:NN], ONESF, nlf[:], True, True, [fb_] + CB, [pb1])
                    pt2, pb2 = bank()
                    mm(pt2[:, 0:NN], TRIU, nlf[:], True, True, [fb_] + CB, [pb2])
                    tr.op("dve", lambda e: e.tensor_copy(Cblk[:], pt1[:, 0:NN]), reads=[pb1], writes=[fb_])
                    for n in range(1, NT):
                        tr.op("dve", lambda e, n=n: e.tensor_tensor(Cblk[:, n * NH:(n + 1) * NH], Cblk[:, n * NH:(n + 1) * NH], Cblk[:, (n - 1) * NH:n * NH], ALU.add),
                              reads=[fb_], writes=[fb_])
                    tr.op("dve", lambda e: e.tensor_copy(cneg[:, 0:NH], pt2[:, 0:NH]), reads=[pb2, fb_], writes=[fb_])
                    if NT > 1:
                        tr.op("dve", lambda e: e.tensor_tensor(cneg[:, NH:NN], pt2[:, NH:NN], Cblk[:, 0:NN - NH], ALU.add), reads=[pb2, fb_], writes=[fb_])

                    tr.op("dve", lambda e: e.tensor_scalar_mul(ngc[:], cneg[:], -1.0), reads=[fb_], writes=[fb_])
                    for i3 in range(3):
                        tr.op("dve", lambda e: e.tensor_copy(PC[:, :, i3], ngc[:]), reads=[fb_], writes=[fb_])
                        if i3 < 2:
                            tr.op("dve", lambda e: e.tensor_tensor(ngc[:], ngc[:], PC[:, :, i3], ALU.subtract), reads=[fb_], writes=[fb_])
                    for pc in range(NP):
                        load_V(c.OFV + pc)
                        def build_cbc(qi):
                            for hh in range(2):
                                h = 2 * pc + hh
                                pt, pb = bank(range(6))
                                for j in range(4):
                                    n = 4 * qi + j
                                    mm(pt[0:3, j * 128:(j + 1) * 128], PC[:, n * NH + h, :], IDENTB, True, True, [fb_] + CB, [pb], signal=(j == 3))
                                tr.op("dve", lambda e: e.tensor_copy(r3[hh][0:3, :], pt[0:3, :]), reads=[pb], writes=[r3b[hh]])
                        wQ, wQb = wload(win_d[l, c.OFQ + pc])
                        wK, wKb = wload(win_d[l, c.OFK + pc])
                        for (w_, wb__, dstT, dstb, gcol) in ((wQ, wQb, qT, qb_, gqs[:, l:l + 1]), (wK, wKb, kT, kb_, prm[:, c.P_GK + l:c.P_GK + l + 1])):
                            for tb in range(NTB):
                                tc_ = slice(tb * 512, (tb + 1) * 512)
                                pQ, pQb = proj(w_, wb__, hnT, [hnb[tb]], tc_, KC)
                                sq, sqb = bfr.get()
                                tr.op("act", lambda e: e.activation(sq[:], pQ, AF.Square), reads=[pQb], writes=[sqb])
                                pt, pb = bank()
                                mm(pt[:], BD, sq[:], True, True, [sqb] + CB, [pb])
                                rs, rsb = f32r.get()
                                tr.op("act", lambda e: e.activation(rs[:], pt[:], AF.Sqrt, bias=epsc[:, 0:1], scale=1.0 / 64), reads=[pb] + CB, writes=[rsb])
                                tr.op("dve", lambda e: e.reciprocal(rs[:], rs[:]), reads=[rsb], writes=[rsb])
                                tr.op("dve", lambda e: e.scalar_tensor_tensor(dstT[:, tc_], pQ, gcol, rs[:], ALU.mult, ALU.mult), reads=[pQb, rsb] + CB, writes=[dstb[tb]])
                        for qi in range(NTB):
                            q0 = qi * 512
                            build_cbc(qi)
                            pO, pOb = ps[6], psb[6]
                            pD, pDb = ps[7], psb[7]
                            nkb = 4 * qi + 4
                            st = {}

                            def stageA(kb):
                                r = kb - 4 * qi
                                c0 = 128 * r if r > 0 else 0
                                diag = r >= 0
                                for hh in range(2):
                                    h = 2 * pc + hh
                                    hs = slice(hh * 64, hh * 64 + 64)
                                    pZ, pZb = bank(range(6))
                                    mm(pZ[:, c0:512], kT[hs, kb * 128:(kb + 1) * 128], qT[hs, q0 + c0:q0 + 512], True, False, [kb_[kb // 4], qb_[qi]], [pZb], signal=False, skip=True)
                                    mm(pZ[:, c0:512], ONES[0:3, 0:128], r3[hh][0:3, c0:512], False, not diag, [r3b[hh]] + CB, [pZb], signal=not diag, skip=True)
                                    if diag:
                                        mm(pZ[:, c0:c0 + 128], IDENTB, NEGM, False, True, CB, [pZb], signal=True, skip=True)
                                    P, Pb = l4.get()
                                    col = kb * NH + h
                                    tr.op("act", lambda e: e.activation(P[:, c0:512], pZ[:, c0:512], AF.Exp, bias=cneg[:, col:col + 1]), reads=[pZb, fb_], writes=[Pb])
                                    st[(kb, hh)] = (P, Pb, c0)

                            def stageB(kb):
                                for hh in range(2):
                                    P, Pb, c0 = st.pop((kb, hh))
                                    hs = slice(hh * 64, hh * 64 + 64)
                                    mm(pO[hs, c0:512], Vp[:, kb, hs], P[:, c0:512], kb == 0, kb == nkb - 1, [Vb, Pb], [pOb], signal=False)
                                    mm(pD[hs, c0:512], ONES[:, 0:64], P[:, c0:512], kb == 0, kb == nkb - 1, [Pb] + CB, [pDb], signal=True)
                            stageA(0)
                            stageA(1)
                            for kb in range(nkb):
                                if kb + 2 < nkb:
                                    stageA(kb + 2)
                                stageB(kb)
                            rd, rdb = f2.get()
                            tr.op("dve", lambda e: e.reciprocal(rd[:], pD[:]), reads=[pDb], writes=[rdb])
                            tr.op("dve", lambda e: e.tensor_tensor(foxT[:, pc, q0:q0 + 512], pO[:], rd[:], ALU.mult), reads=[pOb, rdb], writes=[foxb[pc][qi]])

                    for pc in range(NP):
                        load_V(c.OSV + pc)
                        wQ, wQb = wload(win_d[l, c.OSQ + pc])
                        wK, wKb = wload(win_d[l, c.OSK + pc])
                        for tb in range(NTB):
                            tc_ = slice(tb * 512, (tb + 1) * 512)
                            pQ, pQb = proj(wQ, wQb, hnT, [hnb[tb]], tc_, KC)
                            tr.op("act", lambda e: e.activation(qT[:, tc_], pQ, AF.Copy, scale=0.125), reads=[pQb], writes=[qb_[tb]])
                            pK, pKb = proj(wK, wKb, hnT, [hnb[tb]], tc_, KC)
                            tr.op("dve", lambda e: e.tensor_copy(kT[:, tc_], pK), reads=[pKb], writes=[kb_[tb]])
                        for qi in range(NTB):
                            q0 = qi * 512
                            pO, pOb = ps[6], psb[6]
                            pX = [ps[4], ps[5]]
                            pXb = [psb[4], psb[5]]
                            nkb = 4 * qi + 4
                            for hh in range(2):
                                mm(pX[hh][:], ZEROS, hnT[:, 0, 0:512], True, True, [hnb[0]] + CB, [pXb[hh]], signal=False, skip=True)
                            mm(pO[:], ZEROS, hnT[:, 0, 0:512], True, True, [hnb[0]] + CB, [pOb], signal=False, skip=True)
                            st = {}

                            def sA(kb):
                                r = kb - 4 * qi
                                c0 = 128 * r if r > 0 else 0
                                for hh in range(2):
                                    hs = slice(hh * 64, hh * 64 + 64)
                                    pZ, pZb = bank(range(4))
                                    mm(pZ[:, c0:512], kT[hs, kb * 128:(kb + 1) * 128], qT[hs, q0 + c0:q0 + 512], True, True, [kb_[kb // 4], qb_[qi]], [pZb])
                                    E, Eb = e4.get()
                                    tr.op("act", lambda e: e.activation(E[:, c0:512], pZ[:, c0:512], AF.Exp), reads=[pZb], writes=[Eb])
                                    if r >= 0:
                                        tr.op("dve", lambda e: e.tensor_tensor(E[:, c0:c0 + 128], E[:, c0:c0 + 128], MLT, ALU.mult), reads=[Eb] + CB, writes=[Eb])
                                    Lp, Lpb = l4.get()
                                    tr.op("act", lambda e: e.activation(Lp[:, c0:512], E[:, c0:512], AF.Ln, bias=epsc[:, 1:2]), reads=[Eb], writes=[Lpb])
                                    st[(kb, hh)] = (E, Eb, Lp, Lpb, c0)

                            def sB1(kb):
                                for hh in range(2):
                                    E, Eb, Lp, Lpb, c0 = st[(kb, hh)]
                                    mm(pX[hh][:, c0:512], NTI, Lp[:, c0:512], False, True, [Lpb] + CB, [pXb[hh]], skip=True)
                                fs = []
                                for hh in range(2):
                                    E, Eb, Lp, Lpb, c0 = st[(kb, hh)]
                                    Fx, Fb = f2.get()
                                    tr.op("act", lambda e: e.activation(Fx[:, c0:512], pX[hh][:, c0:512], AF.Exp), reads=[pXb[hh]], writes=[Fb])
                                    fs.append((Fx, Fb))
                                return fs

                            def sB2(kb, fs):
                                for hh in range(2):
                                    E, Eb, Lp, Lpb, c0 = st.pop((kb, hh))
                                    Fx, Fb = fs[hh]
                                    hs = slice(hh * 64, hh * 64 + 64)
                                    mm(pX[hh][:, c0:512], NTS, Lp[:, c0:512], False, True, [Lpb] + CB, [pXb[hh]], signal=False, skip=True)
                                    A, Ab = a2.get()
                                    tr.op("dve", lambda e: e.tensor_tensor(A[:, c0:512], E[:, c0:512], Fx[:, c0:512], ALU.mult), reads=[Eb, Fb], writes=[Ab])
                                    mm(pO[hs, c0:512], Vp[:, kb, hs], A[:, c0:512], False, True, [Vb, Ab], [pOb], skip=True)
                            order = list(range(nkb - 1, -1, -1))
                            sA(order[0])
                            sA(order[1])
                            for i_, kb in enumerate(order):
                                fs = sB1(kb)
                                if i_ + 2 < len(order):
                                    sA(order[i_ + 2])
                                sB2(kb, fs)
                            tr.op("act", lambda e: e.copy(sbT[:, pc, q0:q0 + 512], pO[:]), reads=[pOb], writes=[sbb[pc][qi]])

                tr.fence()
                wring[0] = list(zip(wsl, wslb, wsem))
                with ExitStack() as esg:
                    MG = min(S, 1024)
                    NMB = MG // 512
                    mT = esg.enter_context(nc.sbuf_tensor(f"mT_{l}", [128, KC, MG], BF16))
                    macc = esg.enter_context(nc.sbuf_tensor(f"macc_{l}", [128, MG], F32))
                    mb = [Buf() for _ in range(NMB)]
                    maccb = [Buf() for _ in range(NMB)]
                    extra_slots(esg, f"m{l}")
                    brs = ((convT, convb, wpc_d, NCC), (foxT, foxb, wpf_d, NP), (sbT, sbb, wps_d, NP))
                    for mg in range(S // MG):
                        for oc in range(KC):
                            for br in range(3):
                                srcT, srcb, wd_, nx = brs[br]
                                wp_, wpb_ = wload(wd_[l, oc])
                                wg_, wgb_ = wload(win_d[l, c.OG + br * KC + oc])
                                gcol = c.P_GB + l * 3 * KC + br * KC + oc
                                for j in range(NMB):
                                    tb = mg * NMB + j
                                    tc_ = slice(tb * 512, (tb + 1) * 512)
                                    lc_ = slice(j * 512, (j + 1) * 512)
                                    pY, pYb = proj(wp_, wpb_, srcT, [srcb[k][tb] for k in range(nx)], tc_, nx)
                                    pG, pGb = proj(wg_, wgb_, hnT, [hnb[tb]], tc_, KC)
                                    g, gb = f32r.get()
                                    tr.op("act", lambda e: e.activation(g[:], pG, AF.Sigmoid, bias=prm[:, gcol:gcol + 1]), reads=[pGb] + CB, writes=[gb])
                                    if br == 0:
                                        tr.op("dve", lambda e: e.tensor_tensor(macc[:, lc_], g[:], pY, ALU.mult), reads=[gb, pYb], writes=[maccb[j]])
                                    else:
                                        tr.op("dve", lambda e: e.tensor_tensor(g[:], g[:], pY, ALU.mult), reads=[gb, pYb], writes=[gb])
                                        if br == 1:
                                            tr.op("dve", lambda e: e.tensor_tensor(macc[:, lc_], macc[:, lc_], g[:], ALU.add), reads=[gb, maccb[j]], writes=[maccb[j]])
                                        else:
                                            tr.op("dve", lambda e: e.tensor_tensor(mT[:, oc, lc_], macc[:, lc_], g[:], ALU.add), reads=[gb, maccb[j]], writes=[mb[j]])
                        for oc in range(KC):
                            wo_, wob_ = wload(wout_d[l, oc])
                            for j in range(NMB):
                                tb = mg * NMB + j
                                tc_ = slice(tb * 512, (tb + 1) * 512)
                                lc_ = slice(j * 512, (j + 1) * 512)
                                pR, pRb = proj(wo_, wob_, mT, [mb[j]], lc_, KC)
                                tr.op("dve", lambda e: e.tensor_tensor(xT[:, oc, tc_], xT[:, oc, tc_], pR, ALU.add), reads=[pRb, xTb[oc][tb]], writes=[xTb[oc][tb]])

            tr.fence()
            wring[0] = list(zip(wsl, wslb, wsem))
            norm(c.P_G2 + l * KC)
            with ExitStack() as esf:
                TG, NG = c.TG, c.NG
                NTG = TG // 512
                actT = esf.enter_context(nc.sbuf_tensor(f"actT_{l}", [128, NFF, TG], BF16))
                ug = esf.enter_context(nc.sbuf_tensor(f"ug_{l}", [128, 2 + TG], F32))
                actb = [[Buf() for _ in range(NTG)] for _ in range(NFF)]
                ugb = [Buf() for _ in range(NTG)]
                ug0b = Buf()
                extra_slots(esf, f"f{l}")
                for gi in range(NG):
                    g0 = gi * TG
                    for cf in range(NFF):
                        wG, wGb = wload(wup_d[l, cf])
                        wV, wVb = wload(wup_d[l, NFF + cf])
                        fwc = c.P_FW + (l * NFF + cf) * 3
                        fcc = c.P_FC + l * NFF + cf
                        if gi == 0:
                            tr.op("dve", lambda e: e.memset(ug[:, 0:2], 0.0), writes=[ug0b])
                        else:
                            pt, pb = bank()
                            tbp = (g0 - 2) // 512
                            proj(wG, wGb, hnT, [hnb[tbp]], slice(g0 - 2, g0), KC, out=pt[:, 0:2], outb=pb)
                            tr.op("act", lambda e: e.copy(ug[:, 0:2], pt[:, 0:2]), reads=[pb], writes=[ug0b])
                        for j in range(NTG):
                            tb = gi * NTG + j
                            tc_ = slice(tb * 512, (tb + 1) * 512)
                            o = j * 512
                            pG, pGb = proj(wG, wGb, hnT, [hnb[tb]], tc_, KC)
                            pV, pVb = proj(wV, wVb, hnT, [hnb[tb]], tc_, KC)
                            tr.op("act", lambda e: e.copy(ug[:, 2 + o:2 + o + 512], pG), reads=[pGb], writes=[ugb[j]])
                            prev = [ugb[j - 1]] if j > 0 else [ug0b]
                            t1, t1b = f32r.get()
                            tr.op("dve", lambda e: e.tensor_scalar(t1[:], ug[:, 2 + o:2 + o + 512], prm[:, fwc + 2:fwc + 3], prm[:, fcc:fcc + 1], ALU.mult, ALU.add),
                                  reads=[ugb[j]] + CB, writes=[t1b])
                            tr.op("dve", lambda e: e.scalar_tensor_tensor(t1[:], ug[:, 1 + o:1 + o + 512], prm[:, fwc + 1:fwc + 2], t1[:], ALU.mult, ALU.add),
                                  reads=[ugb[j], t1b] + prev + CB, writes=[t1b])
                            tr.op("dve", lambda e: e.scalar_tensor_tensor(t1[:], ug[:, o:o + 512], prm[:, fwc:fwc + 1], t1[:], ALU.mult, ALU.add),
                                  reads=[ugb[j], t1b] + prev + CB, writes=[t1b])
                            tr.op("act", lambda e: e.activation(t1[:], t1[:], AF.Silu), reads=[t1b], writes=[t1b])
                            tr.op("dve", lambda e: e.tensor_tensor(actT[:, cf, o:o + 512], t1[:], pV, ALU.mult), reads=[t1b, pVb], writes=[actb[cf][j]])
                    for oc in range(KC):
                        halves = []
                        for hk in range(0, NFF, c.HK):
                            n_ = min(c.HK, NFF - hk)
                            wd_, wdb_ = wload(wdn_d[l, oc][:, hk * 128:(hk + n_) * 128])
                            halves.append((hk, n_, wd_, wdb_))
                        for j in range(NTG):
                            tb = gi * NTG + j
                            tc_ = slice(tb * 512, (tb + 1) * 512)
                            pt, pb = bank()
                            for (hk, n_, wd_, wdb_) in halves:
                                for k in range(n_):
                                    cf = hk + k
                                    mm(pt[:], wd_[:, k * 128:(k + 1) * 128], actT[:, cf, j * 512:(j + 1) * 512], cf == 0, cf == NFF - 1,
                                       [wdb_, actb[cf][j]], [pb], signal=(cf == NFF - 1))
                            tr.op("dve", lambda e: e.tensor_tensor(xT[:, oc, tc_], xT[:, oc, tc_], pt[:], ALU.add), reads=[pb, xTb[oc][tb]], writes=[xTb[oc][tb]])

            tr.fence()
            wring[0] = list(zip(wsl, wslb, wsem))

        with ExitStack() as es3:
            yo = [es3.enter_context(nc.sbuf_tensor(f"yo{i}", [128, D], F32)) for i in range(2)]
            yob = [Buf(), Buf()]
            osem = [DmaSem(tr, "out0"), DmaSem(tr, "out1")]
            for n in range(NT):
                k = n % 2
                for k0 in range(0, KC, 4):
                    nk = min(4, KC - k0)
                    pt, pb = bank()
                    for j in range(nk):
                        tr.op("pe", lambda e, j=j: e.transpose(pt[:, j * 128:(j + 1) * 128], xT[:, k0 + j, n * 128:(n + 1) * 128], IDENT),
                              reads=[xTb[k0 + j][n // 4]] + CB, writes=[pb], signal=(j == nk - 1))
                    if (n + k0 // 4) % 2 == 0:
                        tr.op("act", lambda e: e.copy(yo[k][:, k0 * 128:(k0 + nk) * 128], pt[:, 0:nk * 128]), reads=[pb], writes=[yob[k]])
                    else:
                        tr.op("dve", lambda e: e.tensor_copy(yo[k][:, k0 * 128:(k0 + nk) * 128], pt[:, 0:nk * 128]), reads=[pb], writes=[yob[k]])
                tr.dma("sp", y_d[n * 128:(n + 1) * 128, :], yo[k][:], osem[k], reads=[yob[k]])
            for os_ in osem:
                nc.sync.wait_ge(os_.sem, os_.cnt)
        build_nc.stats = (tr.ninst, tr.nsem)
    return nc


_NC_CACHE = {}


def kernel(**inputs):
    cfg = Cfg()
    x = np.asarray(inputs["x"], np.float32)
    B = x.shape[0]
    lay = host_layout(cfg, inputs)
    if "nc" not in _NC_CACHE:
        _NC_CACHE["nc"] = build_nc(cfg)
    nc = _NC_CACHE["nc"]
    in_maps = []
    for b in range(B):
        m = dict(lay)
        m["x"] = np.ascontiguousarray(x[b])
        in_maps.append(m)
    res = run_bass_kernel_spmd(nc, in_maps, core_ids=list(range(B)))
    return np.stack([np.asarray(r["y"], np.float32) for r in res.results], axis=0)
```

```python
import numpy as np
from contextlib import ExitStack
import concourse.bass as bass
import concourse.mybir as mybir
from concourse.bass_utils import run_bass_kernel_spmd

F32 = mybir.dt.float32
BF16 = mybir.dt.bfloat16
AF = mybir.ActivationFunctionType
ALU = mybir.AluOpType
EPS = 1e-6


class Cfg:
    def __init__(self, S=2048, D=1024, DEPTH=4, NP=4, NCC=4, NFF=22, TG=1024):
        self.S, self.D, self.DEPTH, self.NP, self.NCC, self.NFF = S, D, DEPTH, NP, NCC, NFF
        self.KC = D // 128
        self.NT = S // 128
        self.NTB = S // 512
        self.NH = 2 * NP
        self.CW = NCC * 128
        self.FW = NP * 128
        self.DFF = NFF * 128
        self.DIN = 3 * self.CW + 3 * self.FW + self.NH + 3 * self.FW + 3 * D
        self.TG = min(S, TG)
        self.NG = S // self.TG
        self.HK = (NFF + 1) // 2
        self.WSZ = max(self.KC, self.HK, NCC, NP) * 128
        self.OB, self.OC, self.OH = 0, NCC, 2 * NCC
        self.OFQ = 3 * NCC
        self.OFK = self.OFQ + NP
        self.OFV = self.OFK + NP
        self.OSQ = self.OFV + NP
        self.OSK = self.OSQ + NP
        self.OSV = self.OSK + NP
        self.OG = self.OSV + NP
        self.NBLK = self.OG + 3 * self.KC
        o = 0
        self.P_G1 = o; o += DEPTH * self.KC
        self.P_G2 = o; o += DEPTH * self.KC
        self.P_GB = o; o += DEPTH * 3 * self.KC
        self.P_CW = o; o += DEPTH * NCC * 3
        self.P_FB = o; o += DEPTH * self.NT * self.NH
        self.P_GQ = o; o += DEPTH
        self.P_GK = o; o += DEPTH
        self.P_FW = o; o += DEPTH * NFF * 3
        self.P_FC = o; o += DEPTH * NFF
        self.NPRM = o


def _blocks(W, col_starts):
    K = W.shape[0]
    kc = K // 128
    out = np.empty((len(col_starts), 128, kc * 128), np.float32)
    Wr = W.reshape(kc, 128, W.shape[1])
    for i, c0 in enumerate(col_starts):
        out[i] = Wr[:, :, c0:c0 + 128].transpose(1, 0, 2).reshape(128, kc * 128)
    return out


def host_layout(cfg, inp):
    c = cfg
    L = c.DEPTH
    d = {}
    w_in = np.asarray(inp["w_in"], np.float32)
    fcol = 3 * c.CW + 3 * c.FW
    sb0 = fcol + c.NH
    g0 = sb0 + 3 * c.FW
    starts = []
    for grp in range(3):
        starts += [grp * c.CW + i * 128 for i in range(c.NCC)]
    for grp in range(3):
        starts += [3 * c.CW + grp * c.FW + i * 128 for i in range(c.NP)]
    for grp in range(3):
        starts += [sb0 + grp * c.FW + i * 128 for i in range(c.NP)]
    for br in range(3):
        starts += [g0 + br * c.D + i * 128 for i in range(c.KC)]
    assert len(starts) == c.NBLK
    d["win"] = np.stack([_blocks(w_in[l], starts) for l in range(L)])
    wf = w_in[:, :, fcol:fcol + c.NH].reshape(L, c.KC, 128, c.NH).transpose(0, 2, 1, 3)
    d["wf"] = np.ascontiguousarray(wf).reshape(L, 128, c.KC * c.NH)
    oc_starts = [i * 128 for i in range(c.KC)]
    d["wpc"] = np.stack([_blocks(np.asarray(inp["w_proj_conv"][l], np.float32), oc_starts) for l in range(L)])
    d["wpf"] = np.stack([_blocks(np.asarray(inp["w_proj_fox"][l], np.float32), oc_starts) for l in range(L)])
    d["wps"] = np.stack([_blocks(np.asarray(inp["w_proj_sb"][l], np.float32), oc_starts) for l in range(L)])
    d["wout"] = np.stack([_blocks(np.asarray(inp["w_out"][l], np.float32), oc_starts) for l in range(L)])
    d["wup"] = np.stack([_blocks(np.asarray(inp["w_up"][l], np.float32), [i * 128 for i in range(2 * c.NFF)]) for l in range(L)])
    d["wdn"] = np.stack([_blocks(np.asarray(inp["w_down"][l], np.float32), oc_starts) for l in range(L)])
    prm = np.zeros((128, c.NPRM), np.float32)

    def pm(v, n):
        return np.asarray(v, np.float32).reshape(L, n, 128).transpose(2, 0, 1).reshape(128, L * n)
    prm[:, c.P_G1:c.P_G1 + L * c.KC] = pm(inp["norm1_g"], c.KC)
    prm[:, c.P_G2:c.P_G2 + L * c.KC] = pm(inp["norm2_g"], c.KC)
    prm[:, c.P_GB:c.P_GB + L * 3 * c.KC] = pm(inp["gate_bias"], 3 * c.KC)
    cw = np.asarray(inp["conv_w"], np.float32).reshape(L, 3, c.NCC, 128).transpose(3, 0, 2, 1)
    prm[:, c.P_CW:c.P_CW + L * c.NCC * 3] = cw.reshape(128, -1)
    fb = np.asarray(inp["fox_f_bias"], np.float32)
    prm[:, c.P_FB:c.P_FB + L * c.NT * c.NH] = np.broadcast_to(fb[None, :, None, :], (128, L, c.NT, c.NH)).reshape(128, -1)
    gq = np.asarray(inp["fox_q_norm_g"], np.float32)
    gk = np.asarray(inp["fox_k_norm_g"], np.float32)
    prm[:, c.P_GQ:c.P_GQ + L] = np.concatenate([gq, gq], axis=1).T
    prm[:, c.P_GK:c.P_GK + L] = np.concatenate([gk, gk], axis=1).T
    fw = np.asarray(inp["ffn_conv_w"], np.float32).reshape(L, 3, c.NFF, 128).transpose(3, 0, 2, 1)
    prm[:, c.P_FW:c.P_FW + L * c.NFF * 3] = fw.reshape(128, -1)
    prm[:, c.P_FC:c.P_FC + L * c.NFF] = pm(inp["ffn_conv_b"], c.NFF)
    d["prm"] = prm
    i = np.arange(128)
    le = (i[:, None] <= i[None, :]).astype(np.float32)
    lt = (i[:, None] < i[None, :]).astype(np.float32)
    cst = np.zeros((128, 10 * 128), np.float32)
    cst[:, 0:128] = np.eye(128)
    cst[:, 128:256] = le
    cst[:, 256:384] = 1.0
    cst[:, 384:512] = np.where(le > 0, 1e30, -1e4)
    cst[:, 512:640] = lt
    cst[:, 640:768] = 1.0
    bd = np.zeros((128, 128), np.float32); bd[:64, :64] = 1; bd[64:, 64:] = 1
    cst[:, 768:896] = bd
    cst[:, 896:1024] = -(i[:, None] >= i[None, :]).astype(np.float32)
    cst[:, 1024:1152] = -(i[:, None] < i[None, :]).astype(np.float32)
    cst[:, 1152:1280] = 0.0
    d["cst"] = cst
    return d


class Buf:
    __slots__ = ("w", "r", "name")

    def __init__(self, name=""):
        self.w = []
        self.r = []
        self.name = name


class Tracker:
    ROT = 2000

    def __init__(self, nc, es):
        self.nc, self.es = nc, es
        self.E = {"pe": nc.tensor, "act": nc.scalar, "dve": nc.vector, "pool": nc.gpsimd, "sp": nc.sync}
        self.sem, self.cnt, self.gen, self.key = {}, {}, {}, {}
        self.seen = {k: {} for k in self.E}
        self.nsem = 0
        self.ninst = 0
        self.dsems = []
        self.prev = {}
        for k in self.E:
            self._newsem(k)

    def _newsem(self, k):
        g = self.gen.get(k, -1) + 1
        if g > 0:
            self.prev[k] = (self.key[k], self.sem[k], self.cnt[k])
        self.gen[k] = g
        self.sem[k] = self.es.enter_context(self.nc.semaphore(f"s_{k}_{g}"))
        self.cnt[k] = 0
        self.key[k] = f"{k}:{g}"
        self.nsem += 1

    def _waits(self, e, reads, writes):
        need = {}

        def add(tok, raw):
            key, sem, val = tok
            own = key.split(":")[0] == e
            if own and e == "pe":
                return
            if need.get(key, (None, 0))[1] < val:
                need[key] = (sem, val)
        for b in reads:
            for t in b.w:
                add(t, True)
        for b in writes:
            for t in b.w:
                add(t, True)
            for t in b.r:
                add(t, False)
        eng = self.E[e]
        for key, (sem, val) in need.items():
            if self.seen[e].get(key, 0) >= val:
                continue
            eng.wait_ge(sem, val)
            self.seen[e][key] = val

    @staticmethod
    def _addr(b, tok):
        for i, t in enumerate(b.r):
            if t[0] == tok[0]:
                if t[2] < tok[2]:
                    b.r[i] = tok
                return
        b.r.append(tok)

    def op(self, e, fn, reads=(), writes=(), signal=True):
        self._waits(e, reads, writes)
        ins = fn(self.E[e])
        self.ninst += 1
        if signal:
            self.cnt[e] += 1
            ins.then_inc(self.sem[e], 1)
            tok = (self.key[e], self.sem[e], self.cnt[e])
        else:
            tok = (self.key[e], self.sem[e], self.cnt[e] + 1)
        for b in reads:
            self._addr(b, tok)
        for b in writes:
            b.w = [tok]
            b.r = []
        if signal and self.cnt[e] >= self.ROT:
            self._newsem(e)
        return tok

    def fence(self):
        for e, eng in self.E.items():
            toks = []
            for f in self.E:
                if f == e:
                    continue
                if self.cnt[f] > 0:
                    toks.append((self.key[f], self.sem[f], self.cnt[f]))
                elif f in self.prev:
                    toks.append(self.prev[f])
            for d in self.dsems:
                if d.cnt > 0:
                    toks.append((d.key, d.sem, d.cnt))
                elif d.prev is not None:
                    toks.append(d.prev)
            for key, sem, val in toks:
                if self.seen[e].get(key, 0) < val:
                    eng.wait_ge(sem, val)
                    self.seen[e][key] = val

    def dma(self, e, out, in_, dsem, reads=(), writes=()):
        self._waits(e, reads, writes)
        if dsem.cnt >= 1600:
            dsem.rotate()
        ins = self.E[e].dma_start(out=out, in_=in_)
        ins.then_inc(dsem.sem, 16)
        dsem.cnt += 16
        self.ninst += 1
        tok = (dsem.key, dsem.sem, dsem.cnt)
        for b in reads:
            self._addr(b, tok)
        for b in writes:
            b.w = [tok]
            b.r = []
        return tok


class DmaSem:
    _n = 0

    def __init__(self, tr, name):
        self.tr, self.name, self.prev = tr, name, None
        self._new()
        tr.dsems.append(self)

    def _new(self):
        DmaSem._n += 1
        self.sem = self.tr.es.enter_context(self.tr.nc.semaphore(f"d_{self.name}_{DmaSem._n}"))
        self.cnt = 0
        self.key = f"dma{DmaSem._n}:{self.name}"

    def rotate(self):
        self.prev = (self.key, self.sem, self.cnt)
        self._new()


class Ring:
    def __init__(self, nc, es, name, shape, dtype, n):
        self.all = es.enter_context(nc.sbuf_tensor(f"{name}all", [shape[0], n, shape[1]], dtype))
        self.t = [self.all[:, i, :] for i in range(n)]
        self.b = [Buf(f"{name}{i}") for i in range(n)]
        self.i = 0
        self.n = n

    def get_pair(self):
        if self.i % 2:
            self.i += 1
        k = self.i % self.n
        self.i += 2
        return self.all[:, k:k + 2, :], (self.t[k], self.t[k + 1]), (self.b[k], self.b[k + 1])

    def get(self):
        k = self.i % len(self.t)
        self.i += 1
        return self.t[k], self.b[k]


def build_nc(cfg):
    c = cfg
    S, D, L, KC, NT, NTB, NH, NP, NCC, NFF = c.S, c.D, c.DEPTH, c.KC, c.NT, c.NTB, c.NH, c.NP, c.NCC, c.NFF
    nc = bass.Bass("TRN2", target_bir_lowering=False)
    dr = lambda n, sh, kind="ExternalInput": nc.dram_tensor(n, sh, F32, kind=kind).ap()
    x_d = dr("x", [S, D])
    win_d = dr("win", [L, c.NBLK, 128, KC * 128])
    wf_d = dr("wf", [L, 128, KC * NH])
    wpc_d = dr("wpc", [L, KC, 128, NCC * 128])
    wpf_d = dr("wpf", [L, KC, 128, NP * 128])
    wps_d = dr("wps", [L, KC, 128, NP * 128])
    wout_d = dr("wout", [L, KC, 128, KC * 128])
    wup_d = dr("wup", [L, 2 * NFF, 128, KC * 128])
    wdn_d = dr("wdn", [L, KC, 128, NFF * 128])
    prm_d = dr("prm", [128, c.NPRM])
    cst_d = dr("cst", [128, 1280])
    y_d = dr("y", [S, D], "ExternalOutput")

    with ExitStack() as es:
        tr = Tracker(nc, es)
        sb = lambda n, sh, dt: es.enter_context(nc.sbuf_tensor("sb_" + n, sh, dt))
        xT = sb("xT", [128, KC, S], F32)
        hnT = sb("hnT", [128, KC, S], BF16)
        xTb = [[Buf(f"xT{k}_{t}") for t in range(NTB)] for k in range(KC)]
        hnb = [Buf(f"hn{t}") for t in range(NTB)]
        prm = sb("prm", [128, c.NPRM], F32)
        cstf = sb("cstf", [128, 640], F32)
        cstb = sb("cstb", [128, 640], BF16)
        cst_b = Buf("cst")
        IDENT, TRIU, ONESF, CAP, MLT = (cstf[:, i * 128:(i + 1) * 128] for i in range(5))
        ONES, BD, NTI, NTS, ZEROS = (cstb[:, i * 128:(i + 1) * 128] for i in range(5))
        psall = es.enter_context(nc.psum_tensor("psall", [128, 8, 512], F32))
        ps = [psall[:, i, :] for i in range(8)]
        psb = [Buf(f"ps{i}") for i in range(8)]
        psi = [0]

        def bank(allowed=range(8)):
            allowed = list(allowed)
            k = allowed[psi[0] % len(allowed)]
            psi[0] += 1
            return ps[k], psb[k]
        NW = 4
        wsl = [sb(f"wsl{i}", [128, c.WSZ], BF16) for i in range(NW)]
        wslb = [Buf(f"wsl{i}") for i in range(NW)]
        wsem = [DmaSem(tr, f"w{i}") for i in range(NW)]
        wi = [0]

        def wload(src):
            k = wi[0] % NW
            wi[0] += 1
            n = src.shape[1]
            tr.dma("pool", wsl[k][:, 0:n], src, wsem[k], writes=[wslb[k]])
            return wsl[k], wslb[k]
        f32r = Ring(nc, es, "f32r", [128, 512], F32, 6)
        bfr = Ring(nc, es, "bfr", [128, 512], BF16, 6)

        def mm(out, lhsT, rhs, start, stop, reads, writes, signal=True, skip=False):
            if skip:
                return tr.op("pe", lambda e: e.matmul(out, lhsT=lhsT, rhs=rhs, start=start, stop=stop, skip_group_check=True), reads, writes, signal)
            return tr.op("pe", lambda e: e.matmul(out, lhsT=lhsT, rhs=rhs, start=start, stop=stop), reads, writes, signal)

        def proj(w, wb, srcT, srcb, tcols, nk, out=None, outb=None):
            if out is None:
                pt, pb = bank()
                out = pt[:, 0:tcols.stop - tcols.start]
                outb = pb
            for k in range(nk):
                mm(out, w[:, k * 128:(k + 1) * 128], srcT[:, k, tcols], k == 0, k == nk - 1, [wb] + list(srcb), [outb], signal=(k == nk - 1))
            return out, outb

        csem = DmaSem(tr, "cst")
        tr.dma("sp", prm[:], prm_d, csem, writes=[cst_b])
        tr.dma("sp", cstf[:], cst_d[:, 0:640], csem, writes=[cst_b])
        with nc.sbuf_tensor("cb16tmp", [128, 640], F32) as cb16:
            tr.dma("sp", cb16[:], cst_d[:, 640:1280], csem, writes=[cst_b])
            tr.op("dve", lambda e: e.tensor_copy(cstb[:], cb16[:]), reads=[cst_b], writes=[cst_b])
        tr.fence()
        gqs = sb("gqs", [128, L], F32)
        tr.op("dve", lambda e: e.tensor_scalar_mul(gqs[:], prm[:, c.P_GQ:c.P_GQ + L], 0.125), reads=[cst_b], writes=[cst_b])
        epsc = sb("epsc", [128, 2], F32)
        tr.op("dve", lambda e: e.memset(epsc[:, 0:1], EPS), writes=[cst_b])
        tr.op("dve", lambda e: e.memset(epsc[:, 1:2], 1.0), writes=[cst_b])
        CB = [cst_b]

        with ExitStack() as es2:
            xin = [es2.enter_context(nc.sbuf_tensor(f"xin{i}", [128, D], F32)) for i in range(2)]
            xinb = [Buf(), Buf()]
            xsem = [DmaSem(tr, "xin0"), DmaSem(tr, "xin1")]
            for n in range(NT):
                k = n % 2
                tr.dma("sp", xin[k][:], x_d[n * 128:(n + 1) * 128, :], xsem[k], writes=[xinb[k]])
                for k0 in range(0, KC, 4):
                    nk = min(4, KC - k0)
                    pt, pb = bank()
                    for j in range(nk):
                        tr.op("pe", lambda e, j=j: e.transpose(pt[:, j * 128:(j + 1) * 128], xin[k][:, (k0 + j) * 128:(k0 + j + 1) * 128], IDENT),
                              reads=[xinb[k]] + CB, writes=[pb], signal=(j == nk - 1))
                    dst = xT[:, k0:k0 + nk, n * 128:(n + 1) * 128]
                    src = pt[:, 0:nk * 128].rearrange("p (j t) -> p j t", t=128)
                    wb_ = [xTb[k0 + j][n // 4] for j in range(nk)]
                    eng = "act" if (n + k0 // 4) % 2 == 0 else "dve"
                    if eng == "act":
                        tr.op("act", lambda e: e.copy(dst, src), reads=[pb], writes=wb_)
                    else:
                        tr.op("dve", lambda e: e.tensor_copy(dst, src), reads=[pb], writes=wb_)

        tr.fence()

        def norm(gcol0):
            for tb in range(NTB):
                tc_ = slice(tb * 512, (tb + 1) * 512)
                pt, pb = bank()
                for k in range(KC):
                    sq, sqb = bfr.get()
                    tr.op("act", lambda e: e.activation(sq[:], xT[:, k, tc_], AF.Square), reads=[xTb[k][tb]], writes=[sqb])
                    mm(pt[:], ONES, sq[:], k == 0, k == KC - 1, [sqb] + CB, [pb], signal=True)
                rs, rsb = f32r.get()
                tr.op("act", lambda e: e.activation(rs[:], pt[:], AF.Sqrt, bias=epsc[:, 0:1], scale=1.0 / D), reads=[pb] + CB, writes=[rsb])
                tr.op("dve", lambda e: e.reciprocal(rs[:], rs[:]), reads=[rsb], writes=[rsb])
                for k in range(KC):
                    tr.op("dve", lambda e: e.scalar_tensor_tensor(hnT[:, k, tc_], xT[:, k, tc_], prm[:, gcol0 + k:gcol0 + k + 1], rs[:], ALU.mult, ALU.mult),
                          reads=[xTb[k][tb], rsb] + CB, writes=[hnb[tb]])

        HN = hnb

        for l in range(L):
            norm(c.P_G1 + l * KC)
            with ExitStack() as esm:
                sbm = lambda n, sh, dt: esm.enter_context(nc.sbuf_tensor(f"{n}_{l}", sh, dt))
                convT = sbm("convT", [128, NCC, S], BF16)
                foxT = sbm("foxT", [128, NP, S], BF16)
                sbT = sbm("sbT", [128, NP, S], BF16)
                convb = [[Buf() for _ in range(NTB)] for _ in range(NCC)]
                foxb = [[Buf() for _ in range(NTB)] for _ in range(NP)]
                sbb = [[Buf() for _ in range(NTB)] for _ in range(NP)]
                with ExitStack() as esc:
                    u = esc.enter_context(nc.sbuf_tensor(f"u_{l}", [128, 2 + S], F32))
                    ub = [Buf() for _ in range(NTB)]
                    u0b = Buf()
                    tr.op("dve", lambda e: e.memset(u[:, 0:2], 0.0), writes=[u0b])
                    for cc in range(NCC):
                        wB, wBb = wload(win_d[l, c.OB + cc])
                        wC, wCb = wload(win_d[l, c.OC + cc])
                        wH, wHb = wload(win_d[l, c.OH + cc])
                        cwc = c.P_CW + (l * NCC + cc) * 3
                        for tb in range(NTB):
                            tc_ = slice(tb * 512, (tb + 1) * 512)
                            pC, pCb = proj(wC, wCb, hnT, [hnb[tb]], tc_, KC)
                            pH, pHb = proj(wH, wHb, hnT, [hnb[tb]], tc_, KC)
                            pB, pBb = proj(wB, wBb, hnT, [hnb[tb]], tc_, KC)
                            cs, csb = f32r.get()
                            tr.op("act", lambda e: e.copy(cs[:], pC), reads=[pCb], writes=[csb])
                            o = tb * 512
                            tr.op("dve", lambda e: e.tensor_tensor(u[:, 2 + o:2 + o + 512], cs[:], pH, ALU.mult), reads=[csb, pHb], writes=[ub[tb]])
                            t1, t1b = f32r.get()
                            prev = [ub[tb - 1]] if tb > 0 else [u0b]
                            tr.op("dve", lambda e: e.tensor_scalar_mul(t1[:], u[:, 2 + o:2 + o + 512], prm[:, cwc + 2:cwc + 3]), reads=[ub[tb]] + CB, writes=[t1b])
                            tr.op("dve", lambda e: e.scalar_tensor_tensor(t1[:], u[:, 1 + o:1 + o + 512], prm[:, cwc + 1:cwc + 2], t1[:], ALU.mult, ALU.add),
                                  reads=[ub[tb], t1b] + prev + CB, writes=[t1b])
                            tr.op("dve", lambda e: e.scalar_tensor_tensor(t1[:], u[:, o:o + 512], prm[:, cwc:cwc + 1], t1[:], ALU.mult, ALU.add),
                                  reads=[ub[tb], t1b] + prev + CB, writes=[t1b])
                            tr.op("dve", lambda e: e.tensor_tensor(convT[:, cc, tc_], t1[:], pB, ALU.mult), reads=[t1b, pBb], writes=[convb[cc][tb]])

                tr.fence()
                with ExitStack() as esa:
                    sba = lambda n, sh, dt: esa.enter_context(nc.sbuf_tensor(f"{n}_{l}", sh, dt))
                    qT = sba("qT", [128, S], BF16)
                    kT = sba("kT", [128, S], BF16)
                    Vp = sba("Vp", [128, NT, 128], BF16)
                    qb_ = [Buf() for _ in range(NTB)]
                    kb_ = [Buf() for _ in range(NTB)]
                    Vb = Buf()
                    cbc = [sba(f"cbc{i}", [128, 512], F32) for i in range(2)]
                    e4 = f32r
                    f2 = Ring(nc, esa, f"f2_{l}_", [128, 512], F32, 2)
                    l4 = bfr
                    a2 = Ring(nc, esa, f"a2_{l}_", [128, 512], BF16, 2)
                    cbcb = [Buf(), Buf()]
                    NN = NT * NH
                    nlf = sba("nlf", [128, NN], F32)
                    cneg = sba("cneg", [128, NN], F32)
                    Cblk = sba("Cblk", [128, NN], F32)
                    Wt_ = sba("Wtri", [128, NN], F32)
                    xf = sba("xf", [128, NN], F32)
                    wfs = sba("wfs", [128, KC * NH], BF16)
                    tmpT = [sba(f"tmpT{i}", [128, 128], F32) for i in range(2)]
                    tmpTb = [Buf(), Buf()]
                    fb_ = Buf()
                    wfsem = DmaSem(tr, "wf")
                    wfb = Buf()

                    def load_V(blk):
                        wV, wVb = wload(win_d[l, blk])
                        for n0 in range(0, NT, 4):
                            pt, pb = bank()
                            for j in range(4):
                                n = n0 + j
                                for k in range(KC):
                                    mm(pt[:, j * 128:(j + 1) * 128], hnT[:, k, n * 128:(n + 1) * 128], wV[:, k * 128:(k + 1) * 128],
                                       k == 0, k == KC - 1, [wVb, hnb[n // 4]], [pb], signal=(k == KC - 1 and j == 3))
                            tr.op("act", lambda e: e.copy(Vp[:, n0:n0 + 4, :], pt[:].rearrange("p (j t) -> p j t", t=128)), reads=[pb], writes=[Vb])

                    tr.dma("pool", wfs[:], wf_d[l], wfsem, writes=[wfb])
                    pt, pb = bank()
                    for n in range(NT):
                        for k in range(KC):
                            mm(pt[:, n * NH:(n + 1) * NH], hnT[:, k, n * 128:(n + 1) * 128], wfs[:, k * NH:(k + 1) * NH], k == 0, k == KC - 1,
                               [wfb, hnb[n // 4]], [pb], signal=(k == KC - 1 and n == NT - 1))
                    fbc = c.P_FB + l * NN
                    tr.op("dve", lambda e: e.tensor_tensor(xf[:], pt[:, 0:NN], prm[:, fbc:fbc + NN], ALU.add), reads=[pb] + CB, writes=[fb_])
                    tr.op("act", lambda e: e.activation(xf[:], xf[:], AF.Exp, scale=-1.0), reads=[fb_], writes=[fb_])
                    tr.op("act", lambda e: e.activation(nlf[:], xf[:], AF.Ln, bias=epsc[:, 1:2]), reads=[fb_], writes=[fb_])
                    pt1, pb1 = bank()
                    mm(pt1[:, 0:NN], ONESF, nlf[:], True, True, [fb_] + CB, [pb1])
                    pt2, pb2 = bank()
                    mm(pt2[:, 0:NN], TRIU, nlf[:], True, True, [fb_] + CB, [pb2])
                    tr.op("dve", lambda e: e.tensor_copy(Cblk[:], pt1[:, 0:NN]), reads=[pb1], writes=[fb_])
                    for n in range(1, NT):
                        tr.op("dve", lambda e, n=n: e.tensor_tensor(Cblk[:, n * NH:(n + 1) * NH], Cblk[:, n * NH:(n + 1) * NH], Cblk[:, (n - 1) * NH:n * NH], ALU.add),
                              reads=[fb_], writes=[fb_])
                    tr.op("dve", lambda e: e.tensor_copy(cneg[:, 0:NH], pt2[:, 0:NH]), reads=[pb2, fb_], writes=[fb_])
                    if NT > 1:
                        tr.op("dve", lambda e: e.tensor_tensor(cneg[:, NH:NN], pt2[:, NH:NN], Cblk[:, 0:NN - NH], ALU.add), reads=[pb2, fb_], writes=[fb_])

                    for pc in range(NP):
                        load_V(c.OFV + pc)
                        def build_cbc(qi):
                            for hh in range(2):
                                h = 2 * pc + hh
                                pt, pb = bank(range(6))
                                for j in range(4):
                                    n = 4 * qi + j
                                    tt, ttb = tmpT[j % 2], tmpTb[j % 2]
                                    tr.op("dve", lambda e: e.tensor_scalar_mul(tt[:], TRIU, nlf[:, n * NH + h:n * NH + h + 1]), reads=[fb_] + CB, writes=[ttb])
                                    mm(pt[:, j * 128:(j + 1) * 128], ONESF, tt[:], True, True, [ttb] + CB, [pb], signal=True)
                                for j in range(4):
                                    n = 4 * qi + j
                                    if n == 0:
                                        tr.op("dve", lambda e: e.tensor_copy(cbc[hh][:, 0:128], pt[:, 0:128]), reads=[pb], writes=[cbcb[hh]])
                                    else:
                                        col = (n - 1) * NH + h
                                        tr.op("dve", lambda e: e.tensor_scalar_add(cbc[hh][:, j * 128:(j + 1) * 128], pt[:, j * 128:(j + 1) * 128], Cblk[:, col:col + 1]),
                                              reads=[pb, fb_], writes=[cbcb[hh]])
                        wQ, wQb = wload(win_d[l, c.OFQ + pc])
                        wK, wKb = wload(win_d[l, c.OFK + pc])
                        for (w_, wb__, dstT, dstb, gcol) in ((wQ, wQb, qT, qb_, gqs[:, l:l + 1]), (wK, wKb, kT, kb_, prm[:, c.P_GK + l:c.P_GK + l + 1])):
                            for tb in range(NTB):
                                tc_ = slice(tb * 512, (tb + 1) * 512)
                                pQ, pQb = proj(w_, wb__, hnT, [hnb[tb]], tc_, KC)
                                sq, sqb = bfr.get()
                                tr.op("act", lambda e: e.activation(sq[:], pQ, AF.Square), reads=[pQb], writes=[sqb])
                                pt, pb = bank()
                                mm(pt[:], BD, sq[:], True, True, [sqb] + CB, [pb])
                                rs, rsb = f32r.get()
                                tr.op("act", lambda e: e.activation(rs[:], pt[:], AF.Sqrt, bias=epsc[:, 0:1], scale=1.0 / 64), reads=[pb] + CB, writes=[rsb])
                                tr.op("dve", lambda e: e.reciprocal(rs[:], rs[:]), reads=[rsb], writes=[rsb])
                                tr.op("dve", lambda e: e.scalar_tensor_tensor(dstT[:, tc_], pQ, gcol, rs[:], ALU.mult, ALU.mult), reads=[pQb, rsb] + CB, writes=[dstb[tb]])
                        for qi in range(NTB):
                            q0 = qi * 512
                            build_cbc(qi)
                            pO, pOb = ps[6], psb[6]
                            pD, pDb = ps[7], psb[7]
                            nkb = 4 * qi + 4
                            st = {}

                            def stageA(kb):
                                r = kb - 4 * qi
                                c0 = 128 * r if r > 0 else 0
                                for hh in range(2):
                                    h = 2 * pc + hh
                                    hs = slice(hh * 64, hh * 64 + 64)
                                    pZ, pZb = bank(range(6))
                                    mm(pZ[:, c0:512], kT[hs, kb * 128:(kb + 1) * 128], qT[hs, q0 + c0:q0 + 512], True, True, [kb_[kb // 4], qb_[qi]], [pZb])
                                    zs, zsb = e4.get()
                                    tr.op("dve", lambda e: e.tensor_tensor(zs[:, c0:512], pZ[:, c0:512], cbc[hh][:, c0:512], ALU.subtract), reads=[pZb, cbcb[hh]], writes=[zsb])
                                    if r >= 0:
                                        tr.op("dve", lambda e: e.tensor_tensor(zs[:, c0:c0 + 128], zs[:, c0:c0 + 128], CAP, ALU.min), reads=[zsb] + CB, writes=[zsb])
                                    P, Pb = l4.get()
                                    col = kb * NH + h
                                    tr.op("act", lambda e: e.activation(P[:, c0:512], zs[:, c0:512], AF.Exp, bias=cneg[:, col:col + 1]), reads=[zsb, fb_], writes=[Pb])
                                    st[(kb, hh)] = (P, Pb, c0)

                            def stageB(kb):
                                for hh in range(2):
                                    P, Pb, c0 = st.pop((kb, hh))
                                    hs = slice(hh * 64, hh * 64 + 64)
                                    mm(pO[hs, c0:512], Vp[:, kb, hs], P[:, c0:512], kb == 0, kb == nkb - 1, [Vb, Pb], [pOb], signal=False)
                                    mm(pD[hs, c0:512], ONES[:, 0:64], P[:, c0:512], kb == 0, kb == nkb - 1, [Pb] + CB, [pDb], signal=True)
                            stageA(0)
                            stageA(1)
                            for kb in range(nkb):
                                if kb + 2 < nkb:
                                    stageA(kb + 2)
                                stageB(kb)
                            rd, rdb = f2.get()
                            tr.op("dve", lambda e: e.reciprocal(rd[:], pD[:]), reads=[pDb], writes=[rdb])
                            tr.op("dve", lambda e: e.tensor_tensor(foxT[:, pc, q0:q0 + 512], pO[:], rd[:], ALU.mult), reads=[pOb, rdb], writes=[foxb[pc][qi]])

                    for pc in range(NP):
                        load_V(c.OSV + pc)
                        wQ, wQb = wload(win_d[l, c.OSQ + pc])
                        wK, wKb = wload(win_d[l, c.OSK + pc])
                        for tb in range(NTB):
                            tc_ = slice(tb * 512, (tb + 1) * 512)
                            pQ, pQb = proj(wQ, wQb, hnT, [hnb[tb]], tc_, KC)
                            tr.op("act", lambda e: e.activation(qT[:, tc_], pQ, AF.Copy, scale=0.125), reads=[pQb], writes=[qb_[tb]])
                            pK, pKb = proj(wK, wKb, hnT, [hnb[tb]], tc_, KC)
                            tr.op("dve", lambda e: e.tensor_copy(kT[:, tc_], pK), reads=[pKb], writes=[kb_[tb]])
                        for qi in range(NTB):
                            q0 = qi * 512
                            pO, pOb = ps[6], psb[6]
                            pX = [ps[4], ps[5]]
                            pXb = [psb[4], psb[5]]
                            nkb = 4 * qi + 4
                            for hh in range(2):
                                mm(pX[hh][:], ZEROS, hnT[:, 0, 0:512], True, True, [hnb[0]] + CB, [pXb[hh]], signal=False, skip=True)
                            mm(pO[:], ZEROS, hnT[:, 0, 0:512], True, True, [hnb[0]] + CB, [pOb], signal=False, skip=True)
                            st = {}

                            zi = [0]

                            def sA(kb):
                                r = kb - 4 * qi
                                c0 = 128 * r if r > 0 else 0
                                zb0 = 2 * (zi[0] % 2)
                                zi[0] += 1
                                for hh in range(2):
                                    hs = slice(hh * 64, hh * 64 + 64)
                                    mm(ps[zb0 + hh][:, c0:512], kT[hs, kb * 128:(kb + 1) * 128], qT[hs, q0 + c0:q0 + 512], True, True, [kb_[kb // 4], qb_[qi]], [psb[zb0 + hh]])
                                E2, Et, Eb2 = e4.get_pair()
                                tr.op("act", lambda e: e.activation(E2[:, :, c0:512], psall[:, zb0:zb0 + 2, c0:512], AF.Exp), reads=[psb[zb0], psb[zb0 + 1]], writes=list(Eb2))
                                if r >= 0:
                                    for hh in range(2):
                                        tr.op("dve", lambda e: e.tensor_tensor(Et[hh][:, c0:c0 + 128], Et[hh][:, c0:c0 + 128], MLT, ALU.mult), reads=[Eb2[hh]] + CB, writes=[Eb2[hh]])
                                L2, Lt, Lb2 = l4.get_pair()
                                tr.op("act", lambda e: e.activation(L2[:, :, c0:512], E2[:, :, c0:512], AF.Ln, bias=epsc[:, 1:2]), reads=list(Eb2), writes=list(Lb2))
                                st[kb] = (E2, Eb2, Lt, Lb2, c0)

                            def sB1(kb):
                                E2, Eb2, Lt, Lb2, c0 = st[kb]
                                for hh in range(2):
                                    mm(pX[hh][:, c0:512], NTI, Lt[hh][:, c0:512], False, True, [Lb2[hh]] + CB, [pXb[hh]], skip=True)
                                F2, Ft, Fb2 = f2.get_pair()
                                tr.op("act", lambda e: e.activation(F2[:, :, c0:512], psall[:, 4:6, c0:512], AF.Exp), reads=list(pXb), writes=list(Fb2))
                                return (F2, Fb2)

                            def sB2(kb, fs):
                                E2, Eb2, Lt, Lb2, c0 = st.pop(kb)
                                F2, Fb2 = fs
                                for hh in range(2):
                                    mm(pX[hh][:, c0:512], NTS, Lt[hh][:, c0:512], False, True, [Lb2[hh]] + CB, [pXb[hh]], signal=(hh == 1), skip=True)
                                A2, At, Ab2 = a2.get_pair()
                                tr.op("dve", lambda e: e.tensor_tensor(A2[:, :, c0:512], E2[:, :, c0:512], F2[:, :, c0:512], ALU.mult), reads=list(Eb2) + list(Fb2), writes=list(Ab2))
                                for hh in range(2):
                                    hs = slice(hh * 64, hh * 64 + 64)
                                    mm(pO[hs, c0:512], Vp[:, kb, hs], At[hh][:, c0:512], False, True, [Vb, Ab2[hh]], [pOb], signal=(hh == 1), skip=True)
                            order = list(range(nkb - 1, -1, -1))
                            sA(order[0])
                            sA(order[1])
                            for i_, kb in enumerate(order):
                                fs = sB1(kb)
                                if i_ + 2 < len(order):
                                    sA(order[i_ + 2])
                                sB2(kb, fs)
                            tr.op("act", lambda e: e.copy(sbT[:, pc, q0:q0 + 512], pO[:]), reads=[pOb], writes=[sbb[pc][qi]])

                tr.fence()
                with ExitStack() as esg:
                    MG = min(S, 1024)
                    NMB = MG // 512
                    mT = esg.enter_context(nc.sbuf_tensor(f"mT_{l}", [128, KC, MG], BF16))
                    macc = esg.enter_context(nc.sbuf_tensor(f"macc_{l}", [128, MG], F32))
                    mb = [Buf() for _ in range(NMB)]
                    maccb = [Buf() for _ in range(NMB)]
                    brs = ((convT, convb, wpc_d, NCC), (foxT, foxb, wpf_d, NP), (sbT, sbb, wps_d, NP))
                    for mg in range(S // MG):
                        for oc in range(KC):
                            for br in range(3):
                                srcT, srcb, wd_, nx = brs[br]
                                wp_, wpb_ = wload(wd_[l, oc])
                                wg_, wgb_ = wload(win_d[l, c.OG + br * KC + oc])
                                gcol = c.P_GB + l * 3 * KC + br * KC + oc
                                for j in range(NMB):
                                    tb = mg * NMB + j
                                    tc_ = slice(tb * 512, (tb + 1) * 512)
                                    lc_ = slice(j * 512, (j + 1) * 512)
                                    pY, pYb = proj(wp_, wpb_, srcT, [srcb[k][tb] for k in range(nx)], tc_, nx)
                                    pG, pGb = proj(wg_, wgb_, hnT, [hnb[tb]], tc_, KC)
                                    g, gb = f32r.get()
                                    tr.op("act", lambda e: e.activation(g[:], pG, AF.Sigmoid, bias=prm[:, gcol:gcol + 1]), reads=[pGb] + CB, writes=[gb])
                                    if br == 0:
                                        tr.op("dve", lambda e: e.tensor_tensor(macc[:, lc_], g[:], pY, ALU.mult), reads=[gb, pYb], writes=[maccb[j]])
                                    else:
                                        tr.op("dve", lambda e: e.tensor_tensor(g[:], g[:], pY, ALU.mult), reads=[gb, pYb], writes=[gb])
                                        if br == 1:
                                            tr.op("dve", lambda e: e.tensor_tensor(macc[:, lc_], macc[:, lc_], g[:], ALU.add), reads=[gb, maccb[j]], writes=[maccb[j]])
                                        else:
                                            tr.op("dve", lambda e: e.tensor_tensor(mT[:, oc, lc_], macc[:, lc_], g[:], ALU.add), reads=[gb, maccb[j]], writes=[mb[j]])
                        for oc in range(KC):
                            wo_, wob_ = wload(wout_d[l, oc])
                            for j in range(NMB):
                                tb = mg * NMB + j
                                tc_ = slice(tb * 512, (tb + 1) * 512)
                                lc_ = slice(j * 512, (j + 1) * 512)
                                pR, pRb = proj(wo_, wob_, mT, [mb[j]], lc_, KC)
                                tr.op("dve", lambda e: e.tensor_tensor(xT[:, oc, tc_], xT[:, oc, tc_], pR, ALU.add), reads=[pRb, xTb[oc][tb]], writes=[xTb[oc][tb]])

            tr.fence()
            norm(c.P_G2 + l * KC)
            with ExitStack() as esf:
                TG, NG = c.TG, c.NG
                NTG = TG // 512
                actT = esf.enter_context(nc.sbuf_tensor(f"actT_{l}", [128, NFF, TG], BF16))
                ug = esf.enter_context(nc.sbuf_tensor(f"ug_{l}", [128, 2 + TG], F32))
                actb = [[Buf() for _ in range(NTG)] for _ in range(NFF)]
                ugb = [Buf() for _ in range(NTG)]
                ug0b = Buf()
                for gi in range(NG):
                    g0 = gi * TG
                    for cf in range(NFF):
                        wG, wGb = wload(wup_d[l, cf])
                        wV, wVb = wload(wup_d[l, NFF + cf])
                        fwc = c.P_FW + (l * NFF + cf) * 3
                        fcc = c.P_FC + l * NFF + cf
                        if gi == 0:
                            tr.op("dve", lambda e: e.memset(ug[:, 0:2], 0.0), writes=[ug0b])
                        else:
                            pt, pb = bank()
                            tbp = (g0 - 2) // 512
                            proj(wG, wGb, hnT, [hnb[tbp]], slice(g0 - 2, g0), KC, out=pt[:, 0:2], outb=pb)
                            tr.op("act", lambda e: e.copy(ug[:, 0:2], pt[:, 0:2]), reads=[pb], writes=[ug0b])
                        for j in range(NTG):
                            tb = gi * NTG + j
                            tc_ = slice(tb * 512, (tb + 1) * 512)
                            o = j * 512
                            pG, pGb = proj(wG, wGb, hnT, [hnb[tb]], tc_, KC)
                            pV, pVb = proj(wV, wVb, hnT, [hnb[tb]], tc_, KC)
                            tr.op("act", lambda e: e.copy(ug[:, 2 + o:2 + o + 512], pG), reads=[pGb], writes=[ugb[j]])
                            prev = [ugb[j - 1]] if j > 0 else [ug0b]
                            t1, t1b = f32r.get()
                            tr.op("dve", lambda e: e.tensor_scalar(t1[:], ug[:, 2 + o:2 + o + 512], prm[:, fwc + 2:fwc + 3], prm[:, fcc:fcc + 1], ALU.mult, ALU.add),
                                  reads=[ugb[j]] + CB, writes=[t1b])
                            tr.op("dve", lambda e: e.scalar_tensor_tensor(t1[:], ug[:, 1 + o:1 + o + 512], prm[:, fwc + 1:fwc + 2], t1[:], ALU.mult, ALU.add),
                                  reads=[ugb[j], t1b] + prev + CB, writes=[t1b])
                            tr.op("dve", lambda e: e.scalar_tensor_tensor(t1[:], ug[:, o:o + 512], prm[:, fwc:fwc + 1], t1[:], ALU.mult, ALU.add),
                                  reads=[ugb[j], t1b] + prev + CB, writes=[t1b])
                            tr.op("act", lambda e: e.activation(t1[:], t1[:], AF.Silu), reads=[t1b], writes=[t1b])
                            tr.op("dve", lambda e: e.tensor_tensor(actT[:, cf, o:o + 512], t1[:], pV, ALU.mult), reads=[t1b, pVb], writes=[actb[cf][j]])
                    for oc in range(KC):
                        halves = []
                        for hk in range(0, NFF, c.HK):
                            n_ = min(c.HK, NFF - hk)
                            wd_, wdb_ = wload(wdn_d[l, oc][:, hk * 128:(hk + n_) * 128])
                            halves.append((hk, n_, wd_, wdb_))
                        for j in range(NTG):
                            tb = gi * NTG + j
                            tc_ = slice(tb * 512, (tb + 1) * 512)
                            pt, pb = bank()
                            for (hk, n_, wd_, wdb_) in halves:
                                for k in range(n_):
                                    cf = hk + k
                                    mm(pt[:], wd_[:, k * 128:(k + 1) * 128], actT[:, cf, j * 512:(j + 1) * 512], cf == 0, cf == NFF - 1,
                                       [wdb_, actb[cf][j]], [pb], signal=(cf == NFF - 1))
                            tr.op("dve", lambda e: e.tensor_tensor(xT[:, oc, tc_], xT[:, oc, tc_], pt[:], ALU.add), reads=[pb, xTb[oc][tb]], writes=[xTb[oc][tb]])

            tr.fence()

        with ExitStack() as es3:
            yo = [es3.enter_context(nc.sbuf_tensor(f"yo{i}", [128, D], F32)) for i in range(2)]
            yob = [Buf(), Buf()]
            osem = [DmaSem(tr, "out0"), DmaSem(tr, "out1")]
            for n in range(NT):
                k = n % 2
                for k0 in range(0, KC, 4):
                    nk = min(4, KC - k0)
                    pt, pb = bank()
                    for j in range(nk):
                        tr.op("pe", lambda e, j=j: e.transpose(pt[:, j * 128:(j + 1) * 128], xT[:, k0 + j, n * 128:(n + 1) * 128], IDENT),
                              reads=[xTb[k0 + j][n // 4]] + CB, writes=[pb], signal=(j == nk - 1))
                    if (n + k0 // 4) % 2 == 0:
                        tr.op("act", lambda e: e.copy(yo[k][:, k0 * 128:(k0 + nk) * 128], pt[:, 0:nk * 128]), reads=[pb], writes=[yob[k]])
                    else:
                        tr.op("dve", lambda e: e.tensor_copy(yo[k][:, k0 * 128:(k0 + nk) * 128], pt[:, 0:nk * 128]), reads=[pb], writes=[yob[k]])
                tr.dma("sp", y_d[n * 128:(n + 1) * 128, :], yo[k][:], osem[k], reads=[yob[k]])
            for os_ in osem:
                nc.sync.wait_ge(os_.sem, os_.cnt)
        build_nc.stats = (tr.ninst, tr.nsem)
    return nc


_NC_CACHE = {}


def kernel(**inputs):
    cfg = Cfg()
    x = np.asarray(inputs["x"], np.float32)
    B = x.shape[0]
    lay = host_layout(cfg, inputs)
    if "nc" not in _NC_CACHE:
        _NC_CACHE["nc"] = build_nc(cfg)
    nc = _NC_CACHE["nc"]
    in_maps = []
    for b in range(B):
        m = dict(lay)
        m["x"] = np.ascontiguousarray(x[b])
        in_maps.append(m)
    res = run_bass_kernel_spmd(nc, in_maps, core_ids=list(range(B)))
    return np.stack([np.asarray(r["y"], np.float32) for r in res.results], axis=0)
```

```python
import numpy as np
from contextlib import ExitStack
import concourse.bass as bass
import concourse.mybir as mybir
from concourse.bass_utils import run_bass_kernel_spmd

F32 = mybir.dt.float32
BF16 = mybir.dt.bfloat16
AF = mybir.ActivationFunctionType
ALU = mybir.AluOpType
EPS = 1e-6


class Cfg:
    def __init__(self, S=2048, D=1024, DEPTH=4, NP=4, NCC=4, NFF=22, TG=1024):
        self.S, self.D, self.DEPTH, self.NP, self.NCC, self.NFF = S, D, DEPTH, NP, NCC, NFF
        self.KC = D // 128
        self.NT = S // 128
        self.NTB = S // 512
        self.NH = 2 * NP
        self.CW = NCC * 128
        self.FW = NP * 128
        self.DFF = NFF * 128
        self.DIN = 3 * self.CW + 3 * self.FW + self.NH + 3 * self.FW + 3 * D
        self.TG = min(S, TG)
        self.NG = S // self.TG
        self.HK = (NFF + 1) // 2
        self.WSZ = max(self.KC, self.HK, NCC, NP) * 128
        self.OB, self.OC, self.OH = 0, NCC, 2 * NCC
        self.OFQ = 3 * NCC
        self.OFK = self.OFQ + NP
        self.OFV = self.OFK + NP
        self.OSQ = self.OFV + NP
        self.OSK = self.OSQ + NP
        self.OSV = self.OSK + NP
        self.OG = self.OSV + NP
        self.NBLK = self.OG + 3 * self.KC
        o = 0
        self.P_G1 = o; o += DEPTH * self.KC
        self.P_G2 = o; o += DEPTH * self.KC
        self.P_GB = o; o += DEPTH * 3 * self.KC
        self.P_CW = o; o += DEPTH * NCC * 3
        self.P_FB = o; o += DEPTH * self.NT * self.NH
        self.P_GQ = o; o += DEPTH
        self.P_GK = o; o += DEPTH
        self.P_FW = o; o += DEPTH * NFF * 3
        self.P_FC = o; o += DEPTH * NFF
        self.NPRM = o


def _blocks(W, col_starts):
    K = W.shape[0]
    kc = K // 128
    out = np.empty((len(col_starts), 128, kc * 128), np.float32)
    Wr = W.reshape(kc, 128, W.shape[1])
    for i, c0 in enumerate(col_starts):
        out[i] = Wr[:, :, c0:c0 + 128].transpose(1, 0, 2).reshape(128, kc * 128)
    return out


def host_layout(cfg, inp):
    c = cfg
    L = c.DEPTH
    d = {}
    w_in = np.asarray(inp["w_in"], np.float32)
    fcol = 3 * c.CW + 3 * c.FW
    sb0 = fcol + c.NH
    g0 = sb0 + 3 * c.FW
    starts = []
    for grp in range(3):
        starts += [grp * c.CW + i * 128 for i in range(c.NCC)]
    for grp in range(3):
        starts += [3 * c.CW + grp * c.FW + i * 128 for i in range(c.NP)]
    for grp in range(3):
        starts += [sb0 + grp * c.FW + i * 128 for i in range(c.NP)]
    for br in range(3):
        starts += [g0 + br * c.D + i * 128 for i in range(c.KC)]
    assert len(starts) == c.NBLK
    d["win"] = np.stack([_blocks(w_in[l], starts) for l in range(L)])
    wf = w_in[:, :, fcol:fcol + c.NH].reshape(L, c.KC, 128, c.NH).transpose(0, 2, 1, 3)
    d["wf"] = np.ascontiguousarray(wf).reshape(L, 128, c.KC * c.NH)
    oc_starts = [i * 128 for i in range(c.KC)]
    d["wpc"] = np.stack([_blocks(np.asarray(inp["w_proj_conv"][l], np.float32), oc_starts) for l in range(L)])
    d["wpf"] = np.stack([_blocks(np.asarray(inp["w_proj_fox"][l], np.float32), oc_starts) for l in range(L)])
    d["wps"] = np.stack([_blocks(np.asarray(inp["w_proj_sb"][l], np.float32), oc_starts) for l in range(L)])
    d["wout"] = np.stack([_blocks(np.asarray(inp["w_out"][l], np.float32), oc_starts) for l in range(L)])
    d["wup"] = np.stack([_blocks(np.asarray(inp["w_up"][l], np.float32), [i * 128 for i in range(2 * c.NFF)]) for l in range(L)])
    d["wdn"] = np.stack([_blocks(np.asarray(inp["w_down"][l], np.float32), oc_starts) for l in range(L)])
    prm = np.zeros((128, c.NPRM), np.float32)

    def pm(v, n):
        return np.asarray(v, np.float32).reshape(L, n, 128).transpose(2, 0, 1).reshape(128, L * n)
    prm[:, c.P_G1:c.P_G1 + L * c.KC] = pm(inp["norm1_g"], c.KC)
    prm[:, c.P_G2:c.P_G2 + L * c.KC] = pm(inp["norm2_g"], c.KC)
    prm[:, c.P_GB:c.P_GB + L * 3 * c.KC] = pm(inp["gate_bias"], 3 * c.KC)
    cw = np.asarray(inp["conv_w"], np.float32).reshape(L, 3, c.NCC, 128).transpose(3, 0, 2, 1)
    prm[:, c.P_CW:c.P_CW + L * c.NCC * 3] = cw.reshape(128, -1)
    fb = np.asarray(inp["fox_f_bias"], np.float32)
    prm[:, c.P_FB:c.P_FB + L * c.NT * c.NH] = np.broadcast_to(fb[None, :, None, :], (128, L, c.NT, c.NH)).reshape(128, -1)
    gq = np.asarray(inp["fox_q_norm_g"], np.float32)
    gk = np.asarray(inp["fox_k_norm_g"], np.float32)
    prm[:, c.P_GQ:c.P_GQ + L] = np.concatenate([gq, gq], axis=1).T
    prm[:, c.P_GK:c.P_GK + L] = np.concatenate([gk, gk], axis=1).T
    fw = np.asarray(inp["ffn_conv_w"], np.float32).reshape(L, 3, c.NFF, 128).transpose(3, 0, 2, 1)
    prm[:, c.P_FW:c.P_FW + L * c.NFF * 3] = fw.reshape(128, -1)
    prm[:, c.P_FC:c.P_FC + L * c.NFF] = pm(inp["ffn_conv_b"], c.NFF)
    d["prm"] = prm
    i = np.arange(128)
    le = (i[:, None] <= i[None, :]).astype(np.float32)
    lt = (i[:, None] < i[None, :]).astype(np.float32)
    cst = np.zeros((128, 10 * 128), np.float32)
    cst[:, 0:128] = np.eye(128)
    cst[:, 128:256] = le
    cst[:, 256:384] = 1.0
    cst[:, 384:512] = np.where(le > 0, 1e30, -1e4)
    cst[:, 512:640] = lt
    cst[:, 640:768] = 1.0
    bd = np.zeros((128, 128), np.float32); bd[:64, :64] = 1; bd[64:, 64:] = 1
    cst[:, 768:896] = bd
    cst[:, 896:1024] = -(i[:, None] >= i[None, :]).astype(np.float32)
    cst[:, 1024:1152] = -(i[:, None] < i[None, :]).astype(np.float32)
    cst[:, 1152:1280] = 0.0
    d["cst"] = cst
    return d


class Buf:
    __slots__ = ("w", "r", "name")

    def __init__(self, name=""):
        self.w = []
        self.r = []
        self.name = name


class Tracker:
    ROT = 2000

    def __init__(self, nc, es):
        self.nc, self.es = nc, es
        self.E = {"pe": nc.tensor, "act": nc.scalar, "dve": nc.vector, "pool": nc.gpsimd, "sp": nc.sync}
        self.sem, self.cnt, self.gen, self.key = {}, {}, {}, {}
        self.seen = {k: {} for k in self.E}
        self.nsem = 0
        self.ninst = 0
        self.dsems = []
        self.prev = {}
        for k in self.E:
            self._newsem(k)

    def _newsem(self, k):
        g = self.gen.get(k, -1) + 1
        if g > 0:
            self.prev[k] = (self.key[k], self.sem[k], self.cnt[k])
        self.gen[k] = g
        self.sem[k] = self.es.enter_context(self.nc.semaphore(f"s_{k}_{g}"))
        self.cnt[k] = 0
        self.key[k] = f"{k}:{g}"
        self.nsem += 1

    def _waits(self, e, reads, writes):
        need = {}

        def add(tok, raw):
            key, sem, val = tok
            own = key.split(":")[0] == e
            if own and e == "pe":
                return
            if need.get(key, (None, 0))[1] < val:
                need[key] = (sem, val)
        for b in reads:
            for t in b.w:
                add(t, True)
        for b in writes:
            for t in b.w:
                add(t, True)
            for t in b.r:
                add(t, False)
        eng = self.E[e]
        for key, (sem, val) in need.items():
            if self.seen[e].get(key, 0) >= val:
                continue
            eng.wait_ge(sem, val)
            self.seen[e][key] = val

    @staticmethod
    def _addr(b, tok):
        for i, t in enumerate(b.r):
            if t[0] == tok[0]:
                if t[2] < tok[2]:
                    b.r[i] = tok
                return
        b.r.append(tok)

    def op(self, e, fn, reads=(), writes=(), signal=True):
        self._waits(e, reads, writes)
        ins = fn(self.E[e])
        self.ninst += 1
        if signal:
            self.cnt[e] += 1
            ins.then_inc(self.sem[e], 1)
            tok = (self.key[e], self.sem[e], self.cnt[e])
        else:
            tok = (self.key[e], self.sem[e], self.cnt[e] + 1)
        for b in reads:
            self._addr(b, tok)
        for b in writes:
            b.w = [tok]
            b.r = []
        if signal and self.cnt[e] >= self.ROT:
            self._newsem(e)
        return tok

    def fence(self):
        for e, eng in self.E.items():
            toks = []
            for f in self.E:
                if f == e:
                    continue
                if self.cnt[f] > 0:
                    toks.append((self.key[f], self.sem[f], self.cnt[f]))
                elif f in self.prev:
                    toks.append(self.prev[f])
            for d in self.dsems:
                if d.cnt > 0:
                    toks.append((d.key, d.sem, d.cnt))
                elif d.prev is not None:
                    toks.append(d.prev)
            for key, sem, val in toks:
                if self.seen[e].get(key, 0) < val:
                    eng.wait_ge(sem, val)
                    self.seen[e][key] = val

    def dma(self, e, out, in_, dsem, reads=(), writes=()):
        self._waits(e, reads, writes)
        if dsem.cnt >= 1600:
            dsem.rotate()
        ins = self.E[e].dma_start(out=out, in_=in_)
        ins.then_inc(dsem.sem, 16)
        dsem.cnt += 16
        self.ninst += 1
        tok = (dsem.key, dsem.sem, dsem.cnt)
        for b in reads:
            self._addr(b, tok)
        for b in writes:
            b.w = [tok]
            b.r = []
        return tok


class DmaSem:
    _n = 0

    def __init__(self, tr, name):
        self.tr, self.name, self.prev = tr, name, None
        self._new()
        tr.dsems.append(self)

    def _new(self):
        DmaSem._n += 1
        self.sem = self.tr.es.enter_context(self.tr.nc.semaphore(f"d_{self.name}_{DmaSem._n}"))
        self.cnt = 0
        self.key = f"dma{DmaSem._n}:{self.name}"

    def rotate(self):
        self.prev = (self.key, self.sem, self.cnt)
        self._new()


class Ring:
    def __init__(self, nc, es, name, shape, dtype, n):
        self.all = es.enter_context(nc.sbuf_tensor(f"{name}all", [shape[0], n, shape[1]], dtype))
        self.t = [self.all[:, i, :] for i in range(n)]
        self.b = [Buf(f"{name}{i}") for i in range(n)]
        self.i = 0
        self.n = n

    def get_pair(self):
        if self.i % 2:
            self.i += 1
        k = self.i % self.n
        self.i += 2
        return self.all[:, k:k + 2, :], (self.t[k], self.t[k + 1]), (self.b[k], self.b[k + 1])

    def get(self):
        k = self.i % len(self.t)
        self.i += 1
        return self.t[k], self.b[k]


def build_nc(cfg):
    c = cfg
    S, D, L, KC, NT, NTB, NH, NP, NCC, NFF = c.S, c.D, c.DEPTH, c.KC, c.NT, c.NTB, c.NH, c.NP, c.NCC, c.NFF
    nc = bass.Bass("TRN2", target_bir_lowering=False)
    dr = lambda n, sh, kind="ExternalInput": nc.dram_tensor(n, sh, F32, kind=kind).ap()
    x_d = dr("x", [S, D])
    win_d = dr("win", [L, c.NBLK, 128, KC * 128])
    wf_d = dr("wf", [L, 128, KC * NH])
    wpc_d = dr("wpc", [L, KC, 128, NCC * 128])
    wpf_d = dr("wpf", [L, KC, 128, NP * 128])
    wps_d = dr("wps", [L, KC, 128, NP * 128])
    wout_d = dr("wout", [L, KC, 128, KC * 128])
    wup_d = dr("wup", [L, 2 * NFF, 128, KC * 128])
    wdn_d = dr("wdn", [L, KC, 128, NFF * 128])
    prm_d = dr("prm", [128, c.NPRM])
    cst_d = dr("cst", [128, 1280])
    y_d = dr("y", [S, D], "ExternalOutput")

    with ExitStack() as es:
        tr = Tracker(nc, es)
        sb = lambda n, sh, dt: es.enter_context(nc.sbuf_tensor("sb_" + n, sh, dt))
        xT = sb("xT", [128, KC, S], F32)
        hnT = sb("hnT", [128, KC, S], BF16)
        xTb = [[Buf(f"xT{k}_{t}") for t in range(NTB)] for k in range(KC)]
        hnb = [Buf(f"hn{t}") for t in range(NTB)]
        prm = sb("prm", [128, c.NPRM], F32)
        cstf = sb("cstf", [128, 640], F32)
        cstb = sb("cstb", [128, 640], BF16)
        cst_b = Buf("cst")
        IDENT, TRIU, ONESF, CAP, MLT = (cstf[:, i * 128:(i + 1) * 128] for i in range(5))
        ONES, BD, NTI, NTS, ZEROS = (cstb[:, i * 128:(i + 1) * 128] for i in range(5))
        psall = es.enter_context(nc.psum_tensor("psall", [128, 8, 512], F32))
        ps = [psall[:, i, :] for i in range(8)]
        psb = [Buf(f"ps{i}") for i in range(8)]
        psi = [0]

        def bank(allowed=range(8)):
            allowed = list(allowed)
            k = allowed[psi[0] % len(allowed)]
            psi[0] += 1
            return ps[k], psb[k]
        NW = 4
        wsl = [sb(f"wsl{i}", [128, c.WSZ], BF16) for i in range(NW)]
        wslb = [Buf(f"wsl{i}") for i in range(NW)]
        wsem = [DmaSem(tr, f"w{i}") for i in range(NW)]
        wi = [0]

        def wload(src):
            k = wi[0] % NW
            wi[0] += 1
            n = src.shape[1]
            tr.dma("pool", wsl[k][:, 0:n], src, wsem[k], writes=[wslb[k]])
            return wsl[k], wslb[k]
        f32r = Ring(nc, es, "f32r", [128, 512], F32, 6)
        bfr = Ring(nc, es, "bfr", [128, 512], BF16, 6)

        def mm(out, lhsT, rhs, start, stop, reads, writes, signal=True, skip=False):
            if skip:
                return tr.op("pe", lambda e: e.matmul(out, lhsT=lhsT, rhs=rhs, start=start, stop=stop, skip_group_check=True), reads, writes, signal)
            return tr.op("pe", lambda e: e.matmul(out, lhsT=lhsT, rhs=rhs, start=start, stop=stop), reads, writes, signal)

        def proj(w, wb, srcT, srcb, tcols, nk, out=None, outb=None):
            if out is None:
                pt, pb = bank()
                out = pt[:, 0:tcols.stop - tcols.start]
                outb = pb
            for k in range(nk):
                mm(out, w[:, k * 128:(k + 1) * 128], srcT[:, k, tcols], k == 0, k == nk - 1, [wb] + list(srcb), [outb], signal=(k == nk - 1))
            return out, outb

        csem = DmaSem(tr, "cst")
        tr.dma("sp", prm[:], prm_d, csem, writes=[cst_b])
        tr.dma("sp", cstf[:], cst_d[:, 0:640], csem, writes=[cst_b])
        with nc.sbuf_tensor("cb16tmp", [128, 640], F32) as cb16:
            tr.dma("sp", cb16[:], cst_d[:, 640:1280], csem, writes=[cst_b])
            tr.op("dve", lambda e: e.tensor_copy(cstb[:], cb16[:]), reads=[cst_b], writes=[cst_b])
        tr.fence()
        gqs = sb("gqs", [128, L], F32)
        tr.op("dve", lambda e: e.tensor_scalar_mul(gqs[:], prm[:, c.P_GQ:c.P_GQ + L], 0.125), reads=[cst_b], writes=[cst_b])
        epsc = sb("epsc", [128, 2], F32)
        tr.op("dve", lambda e: e.memset(epsc[:, 0:1], EPS), writes=[cst_b])
        tr.op("dve", lambda e: e.memset(epsc[:, 1:2], 1.0), writes=[cst_b])
        CB = [cst_b]

        with ExitStack() as es2:
            xin = [es2.enter_context(nc.sbuf_tensor(f"xin{i}", [128, D], F32)) for i in range(2)]
            xinb = [Buf(), Buf()]
            xsem = [DmaSem(tr, "xin0"), DmaSem(tr, "xin1")]
            for n in range(NT):
                k = n % 2
                tr.dma("sp", xin[k][:], x_d[n * 128:(n + 1) * 128, :], xsem[k], writes=[xinb[k]])
                for k0 in range(0, KC, 4):
                    nk = min(4, KC - k0)
                    pt, pb = bank()
                    for j in range(nk):
                        tr.op("pe", lambda e, j=j: e.transpose(pt[:, j * 128:(j + 1) * 128], xin[k][:, (k0 + j) * 128:(k0 + j + 1) * 128], IDENT),
                              reads=[xinb[k]] + CB, writes=[pb], signal=(j == nk - 1))
                    dst = xT[:, k0:k0 + nk, n * 128:(n + 1) * 128]
                    src = pt[:, 0:nk * 128].rearrange("p (j t) -> p j t", t=128)
                    wb_ = [xTb[k0 + j][n // 4] for j in range(nk)]
                    eng = "act" if (n + k0 // 4) % 2 == 0 else "dve"
                    if eng == "act":
                        tr.op("act", lambda e: e.copy(dst, src), reads=[pb], writes=wb_)
                    else:
                        tr.op("dve", lambda e: e.tensor_copy(dst, src), reads=[pb], writes=wb_)

        tr.fence()

        def norm(gcol0):
            for tb in range(NTB):
                tc_ = slice(tb * 512, (tb + 1) * 512)
                pt, pb = bank()
                for k in range(KC):
                    sq, sqb = bfr.get()
                    tr.op("act", lambda e: e.activation(sq[:], xT[:, k, tc_], AF.Square), reads=[xTb[k][tb]], writes=[sqb])
                    mm(pt[:], ONES, sq[:], k == 0, k == KC - 1, [sqb] + CB, [pb], signal=True)
                rs, rsb = f32r.get()
                tr.op("act", lambda e: e.activation(rs[:], pt[:], AF.Sqrt, bias=epsc[:, 0:1], scale=1.0 / D), reads=[pb] + CB, writes=[rsb])
                tr.op("dve", lambda e: e.reciprocal(rs[:], rs[:]), reads=[rsb], writes=[rsb])
                for k in range(KC):
                    tr.op("dve", lambda e: e.scalar_tensor_tensor(hnT[:, k, tc_], xT[:, k, tc_], prm[:, gcol0 + k:gcol0 + k + 1], rs[:], ALU.mult, ALU.mult),
                          reads=[xTb[k][tb], rsb] + CB, writes=[hnb[tb]])

        HN = hnb

        for l in range(L):
            norm(c.P_G1 + l * KC)
            with ExitStack() as esm:
                sbm = lambda n, sh, dt: esm.enter_context(nc.sbuf_tensor(f"{n}_{l}", sh, dt))
                convT = sbm("convT", [128, NCC, S], BF16)
                foxT = sbm("foxT", [128, NP, S], BF16)
                sbT = sbm("sbT", [128, NP, S], BF16)
                convb = [[Buf() for _ in range(NTB)] for _ in range(NCC)]
                foxb = [[Buf() for _ in range(NTB)] for _ in range(NP)]
                sbb = [[Buf() for _ in range(NTB)] for _ in range(NP)]
                with ExitStack() as esc:
                    u = esc.enter_context(nc.sbuf_tensor(f"u_{l}", [128, 2 + S], F32))
                    ub = [Buf() for _ in range(NTB)]
                    u0b = Buf()
                    tr.op("dve", lambda e: e.memset(u[:, 0:2], 0.0), writes=[u0b])
                    for cc in range(NCC):
                        wB, wBb = wload(win_d[l, c.OB + cc])
                        wC, wCb = wload(win_d[l, c.OC + cc])
                        wH, wHb = wload(win_d[l, c.OH + cc])
                        cwc = c.P_CW + (l * NCC + cc) * 3
                        for tb in range(NTB):
                            tc_ = slice(tb * 512, (tb + 1) * 512)
                            pC, pCb = proj(wC, wCb, hnT, [hnb[tb]], tc_, KC)
                            pH, pHb = proj(wH, wHb, hnT, [hnb[tb]], tc_, KC)
                            pB, pBb = proj(wB, wBb, hnT, [hnb[tb]], tc_, KC)
                            cs, csb = f32r.get()
                            tr.op("act", lambda e: e.copy(cs[:], pC), reads=[pCb], writes=[csb])
                            o = tb * 512
                            tr.op("dve", lambda e: e.tensor_tensor(u[:, 2 + o:2 + o + 512], cs[:], pH, ALU.mult), reads=[csb, pHb], writes=[ub[tb]])
                            t1, t1b = f32r.get()
                            prev = [ub[tb - 1]] if tb > 0 else [u0b]
                            tr.op("dve", lambda e: e.tensor_scalar_mul(t1[:], u[:, 2 + o:2 + o + 512], prm[:, cwc + 2:cwc + 3]), reads=[ub[tb]] + CB, writes=[t1b])
                            tr.op("dve", lambda e: e.scalar_tensor_tensor(t1[:], u[:, 1 + o:1 + o + 512], prm[:, cwc + 1:cwc + 2], t1[:], ALU.mult, ALU.add),
                                  reads=[ub[tb], t1b] + prev + CB, writes=[t1b])
                            tr.op("dve", lambda e: e.scalar_tensor_tensor(t1[:], u[:, o:o + 512], prm[:, cwc:cwc + 1], t1[:], ALU.mult, ALU.add),
                                  reads=[ub[tb], t1b] + prev + CB, writes=[t1b])
                            tr.op("dve", lambda e: e.tensor_tensor(convT[:, cc, tc_], t1[:], pB, ALU.mult), reads=[t1b, pBb], writes=[convb[cc][tb]])

                tr.fence()
                with ExitStack() as esa:
                    sba = lambda n, sh, dt: esa.enter_context(nc.sbuf_tensor(f"{n}_{l}", sh, dt))
                    qT = sba("qT", [128, S], BF16)
                    kT = sba("kT", [128, S], BF16)
                    Vp = sba("Vp", [128, NT, 128], BF16)
                    qb_ = [Buf() for _ in range(NTB)]
                    kb_ = [Buf() for _ in range(NTB)]
                    Vb = Buf()
                    cbc = [sba(f"cbc{i}", [128, 512], F32) for i in range(2)]
                    e4 = f32r
                    f2 = Ring(nc, esa, f"f2_{l}_", [128, 512], F32, 2)
                    l4 = bfr
                    a2 = Ring(nc, esa, f"a2_{l}_", [128, 512], BF16, 2)
                    cbcb = [Buf(), Buf()]
                    NN = NT * NH
                    nlf = sba("nlf", [128, NN], F32)
                    cneg = sba("cneg", [128, NN], F32)
                    Cblk = sba("Cblk", [128, NN], F32)
                    Wt_ = sba("Wtri", [128, NN], F32)
                    xf = sba("xf", [128, NN], F32)
                    wfs = sba("wfs", [128, KC * NH], BF16)
                    tmpT = [sba(f"tmpT{i}", [128, 128], F32) for i in range(2)]
                    tmpTb = [Buf(), Buf()]
                    fb_ = Buf()
                    wfsem = DmaSem(tr, "wf")
                    wfb = Buf()

                    def load_V(blk):
                        wV, wVb = wload(win_d[l, blk])
                        for n0 in range(0, NT, 4):
                            pt, pb = bank()
                            for j in range(4):
                                n = n0 + j
                                for k in range(KC):
                                    mm(pt[:, j * 128:(j + 1) * 128], hnT[:, k, n * 128:(n + 1) * 128], wV[:, k * 128:(k + 1) * 128],
                                       k == 0, k == KC - 1, [wVb, hnb[n // 4]], [pb], signal=(k == KC - 1 and j == 3))
                            tr.op("act", lambda e: e.copy(Vp[:, n0:n0 + 4, :], pt[:].rearrange("p (j t) -> p j t", t=128)), reads=[pb], writes=[Vb])

                    tr.dma("pool", wfs[:], wf_d[l], wfsem, writes=[wfb])
                    pt, pb = bank()
                    for n in range(NT):
                        for k in range(KC):
                            mm(pt[:, n * NH:(n + 1) * NH], hnT[:, k, n * 128:(n + 1) * 128], wfs[:, k * NH:(k + 1) * NH], k == 0, k == KC - 1,
                               [wfb, hnb[n // 4]], [pb], signal=(k == KC - 1 and n == NT - 1))
                    fbc = c.P_FB + l * NN
                    tr.op("dve", lambda e: e.tensor_tensor(xf[:], pt[:, 0:NN], prm[:, fbc:fbc + NN], ALU.add), reads=[pb] + CB, writes=[fb_])
                    tr.op("act", lambda e: e.activation(xf[:], xf[:], AF.Exp, scale=-1.0), reads=[fb_], writes=[fb_])
                    tr.op("act", lambda e: e.activation(nlf[:], xf[:], AF.Ln, bias=epsc[:, 1:2]), reads=[fb_], writes=[fb_])
                    pt1, pb1 = bank()
                    mm(pt1[:, 0:NN], ONESF, nlf[:], True, True, [fb_] + CB, [pb1])
                    pt2, pb2 = bank()
                    mm(pt2[:, 0:NN], TRIU, nlf[:], True, True, [fb_] + CB, [pb2])
                    tr.op("dve", lambda e: e.tensor_copy(Cblk[:], pt1[:, 0:NN]), reads=[pb1], writes=[fb_])
                    for n in range(1, NT):
                        tr.op("dve", lambda e, n=n: e.tensor_tensor(Cblk[:, n * NH:(n + 1) * NH], Cblk[:, n * NH:(n + 1) * NH], Cblk[:, (n - 1) * NH:n * NH], ALU.add),
                              reads=[fb_], writes=[fb_])
                    tr.op("dve", lambda e: e.tensor_copy(cneg[:, 0:NH], pt2[:, 0:NH]), reads=[pb2, fb_], writes=[fb_])
                    if NT > 1:
                        tr.op("dve", lambda e: e.tensor_tensor(cneg[:, NH:NN], pt2[:, NH:NN], Cblk[:, 0:NN - NH], ALU.add), reads=[pb2, fb_], writes=[fb_])

                    for pc in range(NP):
                        load_V(c.OFV + pc)
                        def build_cbc(qi):
                            for hh in range(2):
                                h = 2 * pc + hh
                                pt, pb = bank(range(6))
                                for j in range(4):
                                    n = 4 * qi + j
                                    tt, ttb = tmpT[j % 2], tmpTb[j % 2]
                                    tr.op("dve", lambda e: e.tensor_scalar_mul(tt[:], TRIU, nlf[:, n * NH + h:n * NH + h + 1]), reads=[fb_] + CB, writes=[ttb])
                                    mm(pt[:, j * 128:(j + 1) * 128], ONESF, tt[:], True, True, [ttb] + CB, [pb], signal=True)
                                for j in range(4):
                                    n = 4 * qi + j
                                    if n == 0:
                                        tr.op("dve", lambda e: e.tensor_copy(cbc[hh][:, 0:128], pt[:, 0:128]), reads=[pb], writes=[cbcb[hh]])
                                    else:
                                        col = (n - 1) * NH + h
                                        tr.op("dve", lambda e: e.tensor_scalar_add(cbc[hh][:, j * 128:(j + 1) * 128], pt[:, j * 128:(j + 1) * 128], Cblk[:, col:col + 1]),
                                              reads=[pb, fb_], writes=[cbcb[hh]])
                        wQ, wQb = wload(win_d[l, c.OFQ + pc])
                        wK, wKb = wload(win_d[l, c.OFK + pc])
                        for (w_, wb__, dstT, dstb, gcol) in ((wQ, wQb, qT, qb_, gqs[:, l:l + 1]), (wK, wKb, kT, kb_, prm[:, c.P_GK + l:c.P_GK + l + 1])):
                            for tb in range(NTB):
                                tc_ = slice(tb * 512, (tb + 1) * 512)
                                pQ, pQb = proj(w_, wb__, hnT, [hnb[tb]], tc_, KC)
                                sq, sqb = bfr.get()
                                tr.op("act", lambda e: e.activation(sq[:], pQ, AF.Square), reads=[pQb], writes=[sqb])
                                pt, pb = bank()
                                mm(pt[:], BD, sq[:], True, True, [sqb] + CB, [pb])
                                rs, rsb = f32r.get()
                                tr.op("act", lambda e: e.activation(rs[:], pt[:], AF.Sqrt, bias=epsc[:, 0:1], scale=1.0 / 64), reads=[pb] + CB, writes=[rsb])
                                tr.op("dve", lambda e: e.reciprocal(rs[:], rs[:]), reads=[rsb], writes=[rsb])
                                tr.op("dve", lambda e: e.scalar_tensor_tensor(dstT[:, tc_], pQ, gcol, rs[:], ALU.mult, ALU.mult), reads=[pQb, rsb] + CB, writes=[dstb[tb]])
                        for qi in range(NTB):
                            q0 = qi * 512
                            build_cbc(qi)
                            pO, pOb = ps[6], psb[6]
                            pD, pDb = ps[7], psb[7]
                            nkb = 4 * qi + 4
                            st = {}

                            def stageA(kb):
                                r = kb - 4 * qi
                                c0 = 128 * r if r > 0 else 0
                                for hh in range(2):
                                    h = 2 * pc + hh
                                    hs = slice(hh * 64, hh * 64 + 64)
                                    pZ, pZb = bank(range(6))
                                    mm(pZ[:, c0:512], kT[hs, kb * 128:(kb + 1) * 128], qT[hs, q0 + c0:q0 + 512], True, True, [kb_[kb // 4], qb_[qi]], [pZb])
                                    zs, zsb = e4.get()
                                    tr.op("dve", lambda e: e.tensor_tensor(zs[:, c0:512], pZ[:, c0:512], cbc[hh][:, c0:512], ALU.subtract), reads=[pZb, cbcb[hh]], writes=[zsb])
                                    if r >= 0:
                                        tr.op("dve", lambda e: e.tensor_tensor(zs[:, c0:c0 + 128], zs[:, c0:c0 + 128], CAP, ALU.min), reads=[zsb] + CB, writes=[zsb])
                                    P, Pb = l4.get()
                                    col = kb * NH + h
                                    tr.op("act", lambda e: e.activation(P[:, c0:512], zs[:, c0:512], AF.Exp, bias=cneg[:, col:col + 1]), reads=[zsb, fb_], writes=[Pb])
                                    st[(kb, hh)] = (P, Pb, c0)

                            def stageB(kb):
                                for hh in range(2):
                                    P, Pb, c0 = st.pop((kb, hh))
                                    hs = slice(hh * 64, hh * 64 + 64)
                                    mm(pO[hs, c0:512], Vp[:, kb, hs], P[:, c0:512], kb == 0, kb == nkb - 1, [Vb, Pb], [pOb], signal=False)
                                    mm(pD[hs, c0:512], ONES[:, 0:64], P[:, c0:512], kb == 0, kb == nkb - 1, [Pb] + CB, [pDb], signal=True)
                            stageA(0)
                            stageA(1)
                            for kb in range(nkb):
                                if kb + 2 < nkb:
                                    stageA(kb + 2)
                                stageB(kb)
                            rd, rdb = f2.get()
                            tr.op("dve", lambda e: e.reciprocal(rd[:], pD[:]), reads=[pDb], writes=[rdb])
                            tr.op("dve", lambda e: e.tensor_tensor(foxT[:, pc, q0:q0 + 512], pO[:], rd[:], ALU.mult), reads=[pOb, rdb], writes=[foxb[pc][qi]])

                    for pc in range(NP):
                        load_V(c.OSV + pc)
                        wQ, wQb = wload(win_d[l, c.OSQ + pc])
                        wK, wKb = wload(win_d[l, c.OSK + pc])
                        for tb in range(NTB):
                            tc_ = slice(tb * 512, (tb + 1) * 512)
                            pQ, pQb = proj(wQ, wQb, hnT, [hnb[tb]], tc_, KC)
                            tr.op("act", lambda e: e.activation(qT[:, tc_], pQ, AF.Copy, scale=0.125), reads=[pQb], writes=[qb_[tb]])
                            pK, pKb = proj(wK, wKb, hnT, [hnb[tb]], tc_, KC)
                            tr.op("dve", lambda e: e.tensor_copy(kT[:, tc_], pK), reads=[pKb], writes=[kb_[tb]])
                        for qi in range(NTB):
                            q0 = qi * 512
                            pO, pOb = ps[6], psb[6]
                            pX = [ps[4], ps[5]]
                            pXb = [psb[4], psb[5]]
                            nkb = 4 * qi + 4
                            for hh in range(2):
                                mm(pX[hh][:], ZEROS, hnT[:, 0, 0:512], True, True, [hnb[0]] + CB, [pXb[hh]], signal=False, skip=True)
                            mm(pO[:], ZEROS, hnT[:, 0, 0:512], True, True, [hnb[0]] + CB, [pOb], signal=False, skip=True)
                            st = {}

                            zi = [0]

                            def sA(kb):
                                r = kb - 4 * qi
                                c0 = 128 * r if r > 0 else 0
                                zb0 = 2 * (zi[0] % 2)
                                zi[0] += 1
                                for hh in range(2):
                                    hs = slice(hh * 64, hh * 64 + 64)
                                    mm(ps[zb0 + hh][:, c0:512], kT[hs, kb * 128:(kb + 1) * 128], qT[hs, q0 + c0:q0 + 512], True, True, [kb_[kb // 4], qb_[qi]], [psb[zb0 + hh]])
                                E2, Et, Eb2 = e4.get_pair()
                                tr.op("act", lambda e: e.activation(E2[:, :, c0:512], psall[:, zb0:zb0 + 2, c0:512], AF.Exp), reads=[psb[zb0], psb[zb0 + 1]], writes=list(Eb2))
                                if r >= 0:
                                    for hh in range(2):
                                        tr.op("dve", lambda e: e.tensor_tensor(Et[hh][:, c0:c0 + 128], Et[hh][:, c0:c0 + 128], MLT, ALU.mult), reads=[Eb2[hh]] + CB, writes=[Eb2[hh]])
                                L2, Lt, Lb2 = l4.get_pair()
                                tr.op("act", lambda e: e.activation(L2[:, :, c0:512], E2[:, :, c0:512], AF.Ln, bias=epsc[:, 1:2]), reads=list(Eb2), writes=list(Lb2))
                                st[kb] = (E2, Eb2, Lt, Lb2, c0)

                            def sB1(kb):
                                E2, Eb2, Lt, Lb2, c0 = st[kb]
                                for hh in range(2):
                                    mm(pX[hh][:, c0:512], NTI, Lt[hh][:, c0:512], False, True, [Lb2[hh]] + CB, [pXb[hh]], skip=True)
                                F2, Ft, Fb2 = f2.get_pair()
                                tr.op("act", lambda e: e.activation(F2[:, :, c0:512], psall[:, 4:6, c0:512], AF.Exp), reads=list(pXb), writes=list(Fb2))
                                return (F2, Fb2)

                            def sNTS(kb):
                                E2, Eb2, Lt, Lb2, c0 = st[kb]
                                for hh in range(2):
                                    mm(pX[hh][:, c0:512], NTS, Lt[hh][:, c0:512], False, True, [Lb2[hh]] + CB, [pXb[hh]], signal=(hh == 1), skip=True)

                            def sA_(kb, fs):
                                E2, Eb2, Lt, Lb2, c0 = st[kb]
                                F2, Fb2 = fs
                                A2, At, Ab2 = a2.get_pair()
                                tr.op("dve", lambda e: e.tensor_tensor(A2[:, :, c0:512], E2[:, :, c0:512], F2[:, :, c0:512], ALU.mult), reads=list(Eb2) + list(Fb2), writes=list(Ab2))
                                return At, Ab2

                            def sO(kb, At, Ab2):
                                E2, Eb2, Lt, Lb2, c0 = st.pop(kb)
                                for hh in range(2):
                                    hs = slice(hh * 64, hh * 64 + 64)
                                    mm(pO[hs, c0:512], Vp[:, kb, hs], At[hh][:, c0:512], False, True, [Vb, Ab2[hh]], [pOb], signal=(hh == 1), skip=True)

                            order = list(range(nkb - 1, -1, -1))
                            sA(order[0])
                            sA(order[1])
                            fs = sB1(order[0])
                            for i_, kb in enumerate(order):
                                if i_ + 2 < len(order):
                                    sA(order[i_ + 2])
                                sNTS(kb)
                                At, Ab2 = sA_(kb, fs)
                                if i_ + 1 < len(order):
                                    fs = sB1(order[i_ + 1])
                                sO(kb, At, Ab2)
                            tr.op("act", lambda e: e.copy(sbT[:, pc, q0:q0 + 512], pO[:]), reads=[pOb], writes=[sbb[pc][qi]])

                tr.fence()
                with ExitStack() as esg:
                    MG = min(S, 1024)
                    NMB = MG // 512
                    mT = esg.enter_context(nc.sbuf_tensor(f"mT_{l}", [128, KC, MG], BF16))
                    macc = esg.enter_context(nc.sbuf_tensor(f"macc_{l}", [128, MG], F32))
                    mb = [Buf() for _ in range(NMB)]
                    maccb = [Buf() for _ in range(NMB)]
                    brs = ((convT, convb, wpc_d, NCC), (foxT, foxb, wpf_d, NP), (sbT, sbb, wps_d, NP))
                    for mg in range(S // MG):
                        for oc in range(KC):
                            for br in range(3):
                                srcT, srcb, wd_, nx = brs[br]
                                wp_, wpb_ = wload(wd_[l, oc])
                                wg_, wgb_ = wload(win_d[l, c.OG + br * KC + oc])
                                gcol = c.P_GB + l * 3 * KC + br * KC + oc
                                for j in range(NMB):
                                    tb = mg * NMB + j
                                    tc_ = slice(tb * 512, (tb + 1) * 512)
                                    lc_ = slice(j * 512, (j + 1) * 512)
                                    pY, pYb = proj(wp_, wpb_, srcT, [srcb[k][tb] for k in range(nx)], tc_, nx)
                                    pG, pGb = proj(wg_, wgb_, hnT, [hnb[tb]], tc_, KC)
                                    g, gb = f32r.get()
                                    tr.op("act", lambda e: e.activation(g[:], pG, AF.Sigmoid, bias=prm[:, gcol:gcol + 1]), reads=[pGb] + CB, writes=[gb])
                                    if br == 0:
                                        tr.op("dve", lambda e: e.tensor_tensor(macc[:, lc_], g[:], pY, ALU.mult), reads=[gb, pYb], writes=[maccb[j]])
                                    else:
                                        tr.op("dve", lambda e: e.tensor_tensor(g[:], g[:], pY, ALU.mult), reads=[gb, pYb], writes=[gb])
                                        if br == 1:
                                            tr.op("dve", lambda e: e.tensor_tensor(macc[:, lc_], macc[:, lc_], g[:], ALU.add), reads=[gb, maccb[j]], writes=[maccb[j]])
                                        else:
                                            tr.op("dve", lambda e: e.tensor_tensor(mT[:, oc, lc_], macc[:, lc_], g[:], ALU.add), reads=[gb, maccb[j]], writes=[mb[j]])
                        for oc in range(KC):
                            wo_, wob_ = wload(wout_d[l, oc])
                            for j in range(NMB):
                                tb = mg * NMB + j
                                tc_ = slice(tb * 512, (tb + 1) * 512)
                                lc_ = slice(j * 512, (j + 1) * 512)
                                pR, pRb = proj(wo_, wob_, mT, [mb[j]], lc_, KC)
                                tr.op("dve", lambda e: e.tensor_tensor(xT[:, oc, tc_], xT[:, oc, tc_], pR, ALU.add), reads=[pRb, xTb[oc][tb]], writes=[xTb[oc][tb]])

            tr.fence()
            norm(c.P_G2 + l * KC)
            with ExitStack() as esf:
                TG, NG = c.TG, c.NG
                NTG = TG // 512
                actT = esf.enter_context(nc.sbuf_tensor(f"actT_{l}", [128, NFF, TG], BF16))
                ug = esf.enter_context(nc.sbuf_tensor(f"ug_{l}", [128, 2 + TG], F32))
                actb = [[Buf() for _ in range(NTG)] for _ in range(NFF)]
                ugb = [Buf() for _ in range(NTG)]
                ug0b = Buf()
                for gi in range(NG):
                    g0 = gi * TG
                    for cf in range(NFF):
                        wG, wGb = wload(wup_d[l, cf])
                        wV, wVb = wload(wup_d[l, NFF + cf])
                        fwc = c.P_FW + (l * NFF + cf) * 3
                        fcc = c.P_FC + l * NFF + cf
                        if gi == 0:
                            tr.op("dve", lambda e: e.memset(ug[:, 0:2], 0.0), writes=[ug0b])
                        else:
                            pt, pb = bank()
                            tbp = (g0 - 2) // 512
                            proj(wG, wGb, hnT, [hnb[tbp]], slice(g0 - 2, g0), KC, out=pt[:, 0:2], outb=pb)
                            tr.op("act", lambda e: e.copy(ug[:, 0:2], pt[:, 0:2]), reads=[pb], writes=[ug0b])
                        for j in range(NTG):
                            tb = gi * NTG + j
                            tc_ = slice(tb * 512, (tb + 1) * 512)
                            o = j * 512
                            pG, pGb = proj(wG, wGb, hnT, [hnb[tb]], tc_, KC)
                            pV, pVb = proj(wV, wVb, hnT, [hnb[tb]], tc_, KC)
                            tr.op("act", lambda e: e.copy(ug[:, 2 + o:2 + o + 512], pG), reads=[pGb], writes=[ugb[j]])
                            prev = [ugb[j - 1]] if j > 0 else [ug0b]
                            t1, t1b = f32r.get()
                            tr.op("dve", lambda e: e.tensor_scalar(t1[:], ug[:, 2 + o:2 + o + 512], prm[:, fwc + 2:fwc + 3], prm[:, fcc:fcc + 1], ALU.mult, ALU.add),
                                  reads=[ugb[j]] + CB, writes=[t1b])
                            tr.op("dve", lambda e: e.scalar_tensor_tensor(t1[:], ug[:, 1 + o:1 + o + 512], prm[:, fwc + 1:fwc + 2], t1[:], ALU.mult, ALU.add),
                                  reads=[ugb[j], t1b] + prev + CB, writes=[t1b])
                            tr.op("dve", lambda e: e.scalar_tensor_tensor(t1[:], ug[:, o:o + 512], prm[:, fwc:fwc + 1], t1[:], ALU.mult, ALU.add),
                                  reads=[ugb[j], t1b] + prev + CB, writes=[t1b])
                            tr.op("act", lambda e: e.activation(t1[:], t1[:], AF.Silu), reads=[t1b], writes=[t1b])
                            tr.op("dve", lambda e: e.tensor_tensor(actT[:, cf, o:o + 512], t1[:], pV, ALU.mult), reads=[t1b, pVb], writes=[actb[cf][j]])
                    for oc in range(KC):
                        halves = []
                        for hk in range(0, NFF, c.HK):
                            n_ = min(c.HK, NFF - hk)
                            wd_, wdb_ = wload(wdn_d[l, oc][:, hk * 128:(hk + n_) * 128])
                            halves.append((hk, n_, wd_, wdb_))
                        for j in range(NTG):
                            tb = gi * NTG + j
                            tc_ = slice(tb * 512, (tb + 1) * 512)
                            pt, pb = bank()
                            for (hk, n_, wd_, wdb_) in halves:
                                for k in range(n_):
                                    cf = hk + k
                                    mm(pt[:], wd_[:, k * 128:(k + 1) * 128], actT[:, cf, j * 512:(j + 1) * 512], cf == 0, cf == NFF - 1,
                                       [wdb_, actb[cf][j]], [pb], signal=(cf == NFF - 1))
                            tr.op("dve", lambda e: e.tensor_tensor(xT[:, oc, tc_], xT[:, oc, tc_], pt[:], ALU.add), reads=[pb, xTb[oc][tb]], writes=[xTb[oc][tb]])

            tr.fence()

        with ExitStack() as es3:
            yo = [es3.enter_context(nc.sbuf_tensor(f"yo{i}", [128, D], F32)) for i in range(2)]
            yob = [Buf(), Buf()]
            osem = [DmaSem(tr, "out0"), DmaSem(tr, "out1")]
            for n in range(NT):
                k = n % 2
                for k0 in range(0, KC, 4):
                    nk = min(4, KC - k0)
                    pt, pb = bank()
                    for j in range(nk):
                        tr.op("pe", lambda e, j=j: e.transpose(pt[:, j * 128:(j + 1) * 128], xT[:, k0 + j, n * 128:(n + 1) * 128], IDENT),
                              reads=[xTb[k0 + j][n // 4]] + CB, writes=[pb], signal=(j == nk - 1))
                    if (n + k0 // 4) % 2 == 0:
                        tr.op("act", lambda e: e.copy(yo[k][:, k0 * 128:(k0 + nk) * 128], pt[:, 0:nk * 128]), reads=[pb], writes=[yob[k]])
                    else:
                        tr.op("dve", lambda e: e.tensor_copy(yo[k][:, k0 * 128:(k0 + nk) * 128], pt[:, 0:nk * 128]), reads=[pb], writes=[yob[k]])
                tr.dma("sp", y_d[n * 128:(n + 1) * 128, :], yo[k][:], osem[k], reads=[yob[k]])
            for os_ in osem:
                nc.sync.wait_ge(os_.sem, os_.cnt)
        build_nc.stats = (tr.ninst, tr.nsem)
    return nc


_NC_CACHE = {}


def kernel(**inputs):
    cfg = Cfg()
    x = np.asarray(inputs["x"], np.float32)
    B = x.shape[0]
    lay = host_layout(cfg, inputs)
    if "nc" not in _NC_CACHE:
        _NC_CACHE["nc"] = build_nc(cfg)
    nc = _NC_CACHE["nc"]
    in_maps = []
    for b in range(B):
        m = dict(lay)
        m["x"] = np.ascontiguousarray(x[b])
        in_maps.append(m)
    res = run_bass_kernel_spmd(nc, in_maps, core_ids=list(range(B)))
    return np.stack([np.asarray(r["y"], np.float32) for r in res.results], axis=0)
```
